# Optimizing a Trainium2 kernel written in Bass

```python
import math
import jax, jax.numpy as jnp
from jax import lax
import numpy as np

D_MODEL = 1024
BATCH = 16
SEQ = 2048
DEPTH = 4
DEC_BATCH = 16
DEC_SEQ = 4096
PAST_LEN = 128

N_MIXERS = 2
N_A_LAYERS = (DEPTH + N_MIXERS - 1) // N_MIXERS
N_B_LAYERS = DEPTH // N_MIXERS
HG_HEADS = 8
HG_DK = D_MODEL // HG_HEADS
HG_DV = D_MODEL // HG_HEADS
HG_WIDTH = HG_HEADS * HG_DK
HG_CHUNK = 64
DIL_PATTERNS = ((128, 1), (512, 4), (2048, 16))
N_GROUPS = len(DIL_PATTERNS)
ATT_HEADS = 16
ATT_HD = 64
ATT_WIDTH = ATT_HEADS * ATT_HD
BAND_BLOCK = max(w // (2 * d) for w, d in DIL_PATTERNS)
N_MEM = 256
X_HEADS = 4
X_HD = D_MODEL // X_HEADS
D_FF = 2816
N_NORMS = 9
EPS = 1e-6
NEG = -1e30

kernel_name = 'hgrn2_dilated_attn_hybrid_encoder'


def rmsnorm(x, g):
    x32 = x.astype(jnp.float32)
    y = x32 * lax.rsqrt(jnp.mean(x32 * x32, axis=-1, keepdims=True) + EPS)
    return (y * g.astype(jnp.float32)).astype(x.dtype)


def swiglu(h, w_in, w_out):
    gate, up = jnp.split(h @ w_in, 2, axis=-1)
    return (jax.nn.silu(gate) * up) @ w_out


def alibi_slopes(n):
    return jnp.exp2(-8.0 * jnp.arange(1, n + 1, dtype=jnp.float32) / n)


def forget_gate(z, lb):
    logf = jnp.logaddexp(jnp.log(lb), jnp.log1p(-lb) + jax.nn.log_sigmoid(z))
    k = (1.0 - lb) * jax.nn.sigmoid(-z)
    return logf, k


def _to_chunks(a):
    b, h, l, e = a.shape
    return a.reshape(b, h, l // HG_CHUNK, HG_CHUNK, e).transpose(2, 0, 1, 3, 4)


def hgrn_chunk_scan(q, k, v, logf):
    b, h, l, dk = q.shape
    dv = v.shape[-1]
    lower = jnp.tril(jnp.ones((HG_CHUNK, HG_CHUNK), dtype=bool))

    def step(S, xs):
        qc, kc, vc, lc = xs
        bcum = jnp.cumsum(lc, axis=2)
        btot = bcum[:, :, -1:, :]
        o_inter = jnp.einsum('bhtk,bhkv->bhtv', qc * jnp.exp(bcum), S)
        diff = bcum[:, :, :, None, :] - bcum[:, :, None, :, :]
        decay = jnp.exp(jnp.where(lower[:, :, None], diff, -jnp.inf))
        scores = jnp.einsum('bhtk,bhsk,bhtsk->bhts', qc, kc, decay)
        o_intra = jnp.einsum('bhts,bhsv->bhtv', scores, vc)
        S = jnp.exp(btot[:, :, 0, :])[..., None] * S + jnp.einsum(
            'bhsk,bhsv->bhkv', kc * jnp.exp(btot - bcum), vc)
        return S, o_inter + o_intra

    S0 = jnp.zeros((b, h, dk, dv), jnp.float32)
    _, o = lax.scan(step, S0, (_to_chunks(q), _to_chunks(k), _to_chunks(v), _to_chunks(logf)))
    return o.transpose(1, 2, 0, 3, 4).reshape(b, h, l, dv)


def hgrn2_mixer(h, w_in, lb, gnorm, w_out):
    b, l, _ = h.shape
    q, zf, zb, iv, g = jnp.split(h @ w_in, 5, axis=-1)

    def heads(a):
        return a.astype(jnp.float32).reshape(b, l, HG_HEADS, -1).transpose(0, 2, 1, 3)

    lbh = lb.reshape(HG_HEADS, 1, HG_DK)
    q = heads(jax.nn.silu(q))
    v = heads(iv)
    lf_f, k_f = forget_gate(heads(zf), lbh)
    lf_b, k_b = forget_gate(heads(zb), lbh)
    o_f = hgrn_chunk_scan(q, k_f, v, lf_f)
    rev = lambda a: jnp.flip(a, axis=2)
    o_b = rev(hgrn_chunk_scan(rev(q), rev(k_b), rev(v), rev(lf_b)))
    o = o_f + o_b
    o = o * lax.rsqrt(jnp.mean(o * o, axis=-1, keepdims=True) + EPS)
    o = o * gnorm.astype(jnp.float32) * jax.nn.silu(heads(g))
    o = o.transpose(0, 2, 1, 3).reshape(b, l, HG_WIDTH).astype(h.dtype)
    return o @ w_out


def banded_attention(q, k, v, radius, dilation, slopes):
    z, h, n, hd = q.shape
    nb = -(-n // BAND_BLOCK)
    npad = nb * BAND_BLOCK
    qb = jnp.pad(q, ((0, 0), (0, 0), (0, npad - n), (0, 0))).reshape(z, h, nb, BAND_BLOCK, hd)

    def windows(a):
        a = jnp.pad(a, ((0, 0), (0, 0), (BAND_BLOCK, npad - n + BAND_BLOCK), (0, 0)))
        a = a.reshape(z, h, nb + 2, BAND_BLOCK, hd)
        return jnp.concatenate([a[:, :, :-2], a[:, :, 1:-1], a[:, :, 2:]], axis=3)

    kw, vw = windows(k), windows(v)
    qpos = jnp.arange(npad).reshape(nb, BAND_BLOCK)
    kpos = (jnp.arange(nb) * BAND_BLOCK)[:, None] - BAND_BLOCK + jnp.arange(3 * BAND_BLOCK)[None, :]
    rel = jnp.abs(kpos[:, None, :] - qpos[:, :, None])
    valid = (rel <= radius) & (kpos[:, None, :] >= 0) & (kpos[:, None, :] < n)
    s = jnp.einsum('zhnqd,zhnkd->zhnqk', qb, kw).astype(jnp.float32) * (ATT_HD ** -0.5)
    s = s - slopes[:, None, None, None] * (dilation * rel).astype(jnp.float32)
    s = jnp.where(valid, s, NEG)
    lse = jax.nn.logsumexp(s, axis=-1)
    p = jnp.exp(s - lse[..., None])
    o = jnp.einsum('zhnqk,zhnkd->zhnqd', p, vw.astype(jnp.float32))
    return o.reshape(z, h, npad, hd)[:, :, :n], lse.reshape(z, h, npad)[:, :, :n]


def split_residues(a, dil):
    b, l = a.shape[:2]
    n = l // dil
    return a.reshape(b, n, dil, ATT_HEADS, ATT_HD).transpose(0, 2, 3, 1, 4).reshape(b * dil, ATT_HEADS, n, ATT_HD)


def dilated_attention_mixer(h, w_in, w_out):
    b, l, _ = h.shape
    proj = (h @ w_in).reshape(b, l, N_GROUPS, 3, ATT_HEADS, ATT_HD)
    slopes = alibi_slopes(ATT_HEADS)
    outs, lses = [], []
    for gi, (window, dil) in enumerate(DIL_PATTERNS):
        radius = window // (2 * dil)
        n = l // dil
        q = split_residues(proj[:, :, gi, 0], dil)
        k = split_residues(proj[:, :, gi, 1], dil)
        v = split_residues(proj[:, :, gi, 2], dil)
        o, lse = banded_attention(q, k, v, radius, dil, slopes)
        o = o.reshape(b, dil, ATT_HEADS, n, ATT_HD).transpose(0, 3, 1, 2, 4).reshape(b, l, ATT_HEADS, ATT_HD)
        lse = lse.reshape(b, dil, ATT_HEADS, n).transpose(0, 3, 1, 2).reshape(b, l, ATT_HEADS)
        outs.append(o)
        lses.append(lse)
    alpha = jax.nn.softmax(jnp.stack(lses), axis=0)
    o = jnp.sum(alpha[..., None] * jnp.stack(outs), axis=0)
    return o.reshape(b, l, ATT_WIDTH).astype(h.dtype) @ w_out


def memory_cross_attention(h, m, w_q, w_kv, w_o):
    b, l, _ = h.shape
    q = (h @ w_q).reshape(b, l, X_HEADS, X_HD)
    kv = (m @ w_kv).reshape(b, m.shape[1], 2, X_HEADS, X_HD)
    s = jnp.einsum('blhd,bmhd->bhlm', q, kv[:, :, 0]).astype(jnp.float32) * (X_HD ** -0.5)
    p = jax.nn.softmax(s, axis=-1)
    o = jnp.einsum('bhlm,bmhd->blhd', p, kv[:, :, 1].astype(jnp.float32))
    return o.reshape(b, l, D_MODEL).astype(h.dtype) @ w_o


def run_trunk(x, mem, norm_gains, ffn_w_in, ffn_w_out, hg_w_in, hg_lb_logits, hg_gnorm, hg_w_out,
              att_w_in, att_w_out, xa_w_q, xa_w_kv, xa_w_o):
    p = jax.nn.softmax(hg_lb_logits.astype(jnp.float32), axis=0)
    lower_bounds = jnp.maximum(jnp.cumsum(p, axis=0) - p[0], 0.0)
    for i in range(DEPTH):
        g = norm_gains[i]
        x = x + 0.5 * rmsnorm(swiglu(rmsnorm(x, g[0]), ffn_w_in[i, 0], ffn_w_out[i, 0]), g[1])
        h = rmsnorm(x, g[2])
        j = i // N_MIXERS
        if i % N_MIXERS == 0:
            t = hgrn2_mixer(h, hg_w_in[j], lower_bounds[i], hg_gnorm[j], hg_w_out[j])
        else:
            t = dilated_attention_mixer(h, att_w_in[j], att_w_out[j])
        x = x + rmsnorm(t, g[3])
        c = memory_cross_attention(rmsnorm(x, g[4]), rmsnorm(mem, g[5]), xa_w_q[i], xa_w_kv[i], xa_w_o[i])
        x = x + rmsnorm(c, g[6])
        x = x + 0.5 * rmsnorm(swiglu(rmsnorm(x, g[7]), ffn_w_in[i, 1], ffn_w_out[i, 1]), g[8])
    return x


def setup_inputs(seed: int = 0) -> dict:
    key = jax.random.key(seed)
    ks = jax.random.split(key, 16)
    f32 = jnp.float32

    def w(k, shape, fan_in):
        return jax.random.normal(k, shape, f32) * (fan_in ** -0.5)

    return {
        'x_prompt': jax.random.normal(ks[0], (BATCH, SEQ, D_MODEL), f32),
        'x_sample': jax.random.normal(ks[1], (DEC_BATCH, DEC_SEQ, D_MODEL), f32),
        'mem_prompt': jax.random.normal(ks[2], (BATCH, N_MEM, D_MODEL), f32),
        'mem_sample': jax.random.normal(ks[3], (DEC_BATCH, N_MEM, D_MODEL), f32),
        'norm_gains': 1.0 + 0.05 * jax.random.normal(ks[4], (DEPTH, N_NORMS, D_MODEL), f32),
        'ffn_w_in': w(ks[5], (DEPTH, 2, D_MODEL, 2 * D_FF), D_MODEL),
        'ffn_w_out': w(ks[6], (DEPTH, 2, D_FF, D_MODEL), D_FF),
        'hg_w_in': w(ks[7], (N_A_LAYERS, D_MODEL, 5 * HG_WIDTH), D_MODEL),
        'hg_lb_logits': 0.5 * jax.random.normal(ks[8], (DEPTH, HG_WIDTH), f32),
        'hg_gnorm': 1.0 + 0.05 * jax.random.normal(ks[9], (N_A_LAYERS, HG_DV), f32),
        'hg_w_out': w(ks[10], (N_A_LAYERS, HG_WIDTH, D_MODEL), HG_WIDTH),
        'att_w_in': w(ks[11], (N_B_LAYERS, D_MODEL, N_GROUPS * 3 * ATT_WIDTH), D_MODEL),
        'att_w_out': w(ks[12], (N_B_LAYERS, ATT_WIDTH, D_MODEL), ATT_WIDTH),
        'xa_w_q': w(ks[13], (DEPTH, D_MODEL, D_MODEL), D_MODEL),
        'xa_w_kv': w(ks[14], (DEPTH, D_MODEL, 2 * D_MODEL), D_MODEL),
        'xa_w_o': w(ks[15], (DEPTH, D_MODEL, D_MODEL), D_MODEL),
    }


def reference(x_prompt, x_sample, mem_prompt, mem_sample, norm_gains, ffn_w_in, ffn_w_out,
              hg_w_in, hg_lb_logits, hg_gnorm, hg_w_out, att_w_in, att_w_out, xa_w_q, xa_w_kv, xa_w_o):
    y_prompt = run_trunk(x_prompt, mem_prompt, norm_gains, ffn_w_in, ffn_w_out, hg_w_in, hg_lb_logits,
                         hg_gnorm, hg_w_out, att_w_in, att_w_out, xa_w_q, xa_w_kv, xa_w_o)
    y_sample = run_trunk(x_sample, mem_sample, norm_gains, ffn_w_in, ffn_w_out, hg_w_in, hg_lb_logits,
                         hg_gnorm, hg_w_out, att_w_in, att_w_out, xa_w_q, xa_w_kv, xa_w_o)
    return (y_prompt, y_sample)
```

```python
import numpy as np
import ml_dtypes
import concourse.bass as bass
import concourse.mybir as mybir
from concourse.bass_utils import run_bass_kernel_spmd

F32 = mybir.dt.float32
BF16 = mybir.dt.bfloat16
AF = mybir.ActivationFunctionType
ALU = mybir.AluOpType
AX = mybir.AxisListType

D = 1024
DFF = 2816
NCH = DFF // 128
EPS = 1e-6
N_MEM = 256
ENGS = ("pe", "act", "dve", "pool", "sp")
EPOCH = 30000


class Op:
    __slots__ = ("eng", "fn", "dma", "sem", "deps", "idx", "signal", "sig_no", "waits", "dval")

    def __init__(self, eng, fn, dma, sem):
        self.eng = eng
        self.fn = fn
        self.dma = dma
        self.sem = sem
        self.deps = ()
        self.signal = False
        self.sig_no = None
        self.waits = None
        self.dval = None


class Rec:
    def __init__(self):
        self.ops = {e: [] for e in ENGS}
        self.last_w = {}
        self.readers = {}
        self.dma_count = {}

    def add(self, eng, meth, r=(), w=(), dma=False, sem=None, **kw):
        op = Op(eng, (meth, kw), dma, sem)
        deps = []
        for k in r:
            lw = self.last_w.get(k)
            if lw is not None:
                deps.append(lw)
        for k in w:
            lw = self.last_w.get(k)
            if lw is not None:
                deps.append(lw)
            rd = self.readers.get(k)
            if rd:
                for v in rd.values():
                    if isinstance(v, list):
                        deps.extend(v)
                    else:
                        deps.append(v)
        op.deps = deps
        op.idx = len(self.ops[eng])
        self.ops[eng].append(op)
        if dma:
            assert sem is not None
            c = self.dma_count.get(sem, 0) + 1
            self.dma_count[sem] = c
            op.dval = 16 * c
        for k in w:
            self.last_w[k] = op
            self.readers[k] = {}
        for k in r:
            d = self.readers.setdefault(k, {})
            if dma:
                d.setdefault(("dma", eng), []).append(op)
            else:
                d[eng] = op
        return op

    def emit(self, nc, name):
        for e in ENGS:
            seen = {}
            seen_dma = {}
            for op in self.ops[e]:
                waits_c = {}
                waits_d = {}
                for d in op.deps:
                    if d is op:
                        continue
                    if d.dma:
                        if seen_dma.get(d.sem, 0) >= d.dval:
                            continue
                        if waits_d.get(d.sem, 0) < d.dval:
                            waits_d[d.sem] = d.dval
                    else:
                        if d.eng == e:
                            if e == "pe" or e == "sp":
                                continue
                            if d.idx >= op.idx:
                                continue
                        if seen.get(d.eng, -1) >= d.idx:
                            continue
                        if waits_c.get(d.eng, (-1, None))[0] < d.idx:
                            waits_c[d.eng] = (d.idx, d)
                for pe_, (i, d) in waits_c.items():
                    seen[pe_] = i
                    d.signal = True
                for s, v in waits_d.items():
                    seen_dma[s] = v
                op.waits = (waits_c, waits_d)
        nsig = {}
        for e in ENGS:
            n = 0
            for op in self.ops[e]:
                if op.signal and not op.dma:
                    op.sig_no = n
                    n += 1
            nsig[e] = n
        allsems = []

        def mk(nm):
            h = nc.alloc_semaphore(nm)
            allsems.append(h)
            return h

        if True:
            csem = {}
            for e in ENGS:
                csem[e] = [mk(f"{name}_c_{e}_{i}") for i in range((nsig[e] + EPOCH - 1) // EPOCH)]
            dsem = {s: mk(f"{name}_d_{s}") for s in self.dma_count}
        with nc.Block() as block:

            def run(e, eng):
                for op in self.ops[e]:
                    wc, wd = op.waits
                    for pe_, (i, d) in wc.items():
                        eng.wait_ge(csem[pe_][d.sig_no // EPOCH], d.sig_no % EPOCH + 1)
                    for s, v in wd.items():
                        eng.wait_ge(dsem[s], v)
                    ins = getattr(eng, op.fn[0])(**op.fn[1])
                    if op.dma:
                        ins.then_inc(dsem[op.sem], 16)
                    elif op.signal:
                        ins.then_inc(csem[e][op.sig_no // EPOCH], 1)
                mine = {}
                for op in self.ops[e]:
                    if op.dma:
                        mine[op.sem] = max(mine.get(op.sem, 0), op.dval)
                for s, v in mine.items():
                    eng.wait_ge(dsem[s], v)

            if self.ops["pe"]:
                block.tensor(lambda eng: run("pe", eng))
            if self.ops["act"]:
                block.scalar(lambda eng: run("act", eng))
            if self.ops["dve"]:
                block.vector(lambda eng: run("dve", eng))
            if self.ops["pool"]:
                block.gpsimd(lambda eng: run("pool", eng))
            if self.ops["sp"]:
                block.sync(lambda eng: run("sp", eng))
        nc.all_engine_barrier()
        nc.clear_and_free_semaphores(allsems)
        nc.all_engine_barrier()


class Ctx:
    def __init__(self, nc, name, stack):
        self.nc = nc
        self.name = name
        self.st = stack
        self.rec = Rec()
        self.n = 0

    def sb(self, shape, dt, name=None):
        self.n += 1
        t = self.st.enter_context(self.nc.sbuf_tensor(f"{self.name}_{name or 't'}{self.n}", list(shape), dt))
        return t

    def ps(self, shape, dt=F32, name=None):
        self.n += 1
        return self.st.enter_context(self.nc.psum_tensor(f"{self.name}_{name or 'p'}{self.n}", list(shape), dt))


def col_view(vec_ap):
    return vec_ap.rearrange("(c p) -> p c", p=128)


class TokenStage:
    def __init__(self, cx, consts, src, dst, g_pre_ap, g_post_ap, out_scale, tb=512, pre_from_bf16=False, no_post=False):
        self.cx = cx
        self.rec = cx.rec
        self.src = src
        self.dst = dst
        self.tb = tb
        self.tpb = tb // 128
        self.out_scale = out_scale
        self.ident = consts["ident"]
        rec = self.rec
        self.pre_from_bf16 = pre_from_bf16
        if not pre_from_bf16:
            self.xpre = [cx.sb([128, D], F32, "xpre") for _ in range(2)]
        if not no_post:
            self.xpost = [cx.sb([128, D], F32, "xpost") for _ in range(2)]
            self.tmp = cx.sb([128, D], F32, "tmp")
            self.gpost = cx.sb([128, D], F32, "gpost")
        self.xsb = [cx.sb([128, D], BF16, "xsb") for _ in range(2)]
        self.xnT = [cx.sb([128, 8, tb], BF16, "xnT") for _ in range(2)]
        self.sq = cx.sb([128, D], BF16, "sqjunk")
        self.stat = cx.sb([128, 16], F32, "stat")
        self.gpre = cx.sb([128, 8], F32, "gpre")
        self.psT = cx.ps([128, 8, 128], BF16, "psT")
        self.has_pre_norm = g_pre_ap is not None
        if g_pre_ap is not None:
            rec.add("sp", "dma_start", out=self.gpre[:], in_=col_view(g_pre_ap), allow_slow_non_contiguous=True,
                    w=["gpre"], dma=True, sem="gpre")
        if not no_post:
            rec.add("sp", "dma_start", out=self.gpost[:], in_=g_post_ap.partition_broadcast(128),
                    w=["gpost"], dma=True, sem="gpost")
        self.npre = 0
        self.npost = 0

    def rstd(self, ss_ap, out_ap, scale, rkeys, wkeys):
        rec = self.rec
        sc2 = float(scale) ** 2
        rec.add("act", "activation", out=out_ap, in_=ss_ap, func=AF.Sqrt, scale=1.0 / (D * sc2), bias=EPS / sc2,
                r=rkeys, w=wkeys)
        rec.add("dve", "reciprocal", out=out_ap, in_=out_ap, r=wkeys, w=wkeys)

    def pre_tile(self, g, src=None):
        rec = self.rec
        i = self.npre
        self.npre += 1
        s = i % 2
        src = self.src if src is None else src
        rows = src[g * 128:(g + 1) * 128, :]
        xb = self.xsb[s]
        if self.pre_from_bf16:
            rec.add("sp", "dma_start", out=xb[:], in_=rows, w=[("xsb", s)], dma=True, sem=f"xpre{s}")
            return s
        xt = self.xpre[s]
        rec.add("sp", "dma_start", out=xt[:], in_=rows, w=[("xpre", s)], dma=True, sem=f"xpre{s}")
        ss = self.stat[:, s:s + 1]
        rs = self.stat[:, 2 + s:3 + s]
        rec.add("act", "activation", out=self.sq[:], in_=xt[:], func=AF.Square, accum_out=ss,
                r=[("xpre", s)], w=[("sq", 0), ("sq", 1), ("ss", s)])
        self.rstd(ss, rs, 1.0, [("ss", s)], [("rs", s)])
        rec.add("act", "activation", out=xb[:], in_=xt[:], func=AF.Copy, scale=rs,
                r=[("xpre", s), ("rs", s)], w=[("xsb", s)])
        return s

    def transpose_tile(self, s, blk_slot, t):
        rec = self.rec
        xb = self.xsb[s]
        pT = self.psT
        for c in range(8):
            rec.add("pe", "transpose", out=pT[:, c, :], in_=xb[:, c * 128:(c + 1) * 128], identity=self.ident[:],
                    r=[("xsb", s), "ident"], w=["psT"])
        dstT = self.xnT[blk_slot][:, :, t * 128:(t + 1) * 128]
        if self.has_pre_norm:
            gb = self.gpre[:].unsqueeze(2).to_broadcast([128, 8, 128])
            rec.add("dve", "tensor_tensor", out=dstT, in0=pT[:], in1=gb, op=ALU.mult,
                    r=["psT", "gpre"], w=[("xnT", blk_slot, t)])
        else:
            rec.add("dve", "tensor_copy", out=dstT, in_=pT[:], r=["psT"], w=[("xnT", blk_slot, t)])

    def pre_block(self, b, slot, src=None):
        for ph in self.pre_block_phases(b, slot, src):
            ph()

    def pre_block_phases(self, b, slot, src=None):
        st = {}
        g0 = b * self.tpb

        def p0():
            st[0] = self.pre_tile(g0 + 0, src)
            st[1] = self.pre_tile(g0 + 1, src)

        def p1():
            self.transpose_tile(st[0], slot, 0)
            st[2] = self.pre_tile(g0 + 2, src)
            self.transpose_tile(st[1], slot, 1)
            st[3] = self.pre_tile(g0 + 3, src)

        def p2():
            self.transpose_tile(st[2], slot, 2)
            self.transpose_tile(st[3], slot, 3)

        return [p0, p1, p2]

    def xnT_keys(self, slot):
        return [("xnT", slot, t) for t in range(self.tpb)]

    def post_load(self, g):
        rec = self.rec
        i = self.npost
        self.npost += 1
        s = i % 2
        rows = self.src[g * 128:(g + 1) * 128, :]
        rec.add("sp", "dma_start", out=self.xpost[s][:], in_=rows, w=[("xpost", s)], dma=True, sem=f"xpost{s}")
        return s

    def post_tile(self, g, s, ps_out, ps_key, pkeys=None):
        rec = self.rec
        xt = self.xpost[s]
        ss = self.stat[:, 4 + s:5 + s]
        rs = self.stat[:, 6 + s:7 + s]
        pk = [(ps_key, 0), (ps_key, 1)] if pkeys is None else list(pkeys)
        rec.add("act", "activation", out=self.sq[:], in_=ps_out, func=AF.Square, accum_out=ss,
                r=pk, w=[("sq", 0), ("sq", 1), ("pss", s)])
        self.rstd(ss, rs, self.out_scale, [("pss", s)], [("prs", s)])
        rec.add("dve", "tensor_tensor", out=self.tmp[:], in0=ps_out, in1=self.gpost[:], op=ALU.mult,
                r=pk + ["gpost"], w=["tmp"])
        rec.add("dve", "scalar_tensor_tensor", out=xt[:], in0=self.tmp[:], scalar=rs, in1=xt[:],
                op0=ALU.mult, op1=ALU.add, r=["tmp", ("prs", s), ("xpost", s)], w=[("xpost", s)])
        rows = self.dst[g * 128:(g + 1) * 128, :]
        rec.add("sp", "dma_start", out=rows, in_=xt[:], r=[("xpost", s)], dma=True, sem=f"xst{s}")


def load_w_rows(rec, wt, w_ap, key, nk, col_splits=1):
    wv = w_ap.rearrange("(c p) f -> p c f", p=128)
    F = wv.shape[2]
    step = F // col_splits
    for q in range(col_splits):
        for c in range(nk):
            rec.add("pool", "dma_start", out=wt[:, c, q * step:(q + 1) * step], in_=wv[:, c, q * step:(q + 1) * step],
                    w=[(key, c, q)], dma=True, sem=f"{key}_{c}_{q}")


def ffn_stage(nc, name, consts, src, dst, nt, w_in, w_out, g_pre, g_post, dbg=None):
    import contextlib
    with contextlib.ExitStack() as st:
        cx = Ctx(nc, name, st)
        rec = cx.rec
        ts = TokenStage(cx, consts, src, dst, g_pre, g_post, 0.5)
        tb, tpb = ts.tb, ts.tpb
        nblk = nt // tb
        win = cx.sb([128, 8, 2 * DFF], BF16, "win")
        wout = cx.sb([128, NCH, D], BF16, "wout")
        aT = cx.sb([128, NCH, tb], BF16, "aT")
        gs = [cx.sb([128, tb], BF16, "gs") for _ in range(2)]
        psGU = cx.ps([128, 4, tb], F32, "psGU")
        psG = [psGU[:, 0, :], psGU[:, 1, :]]
        psU = [psGU[:, 2, :], psGU[:, 3, :]]
        psO = cx.ps([128, D], F32, "psO")
        alt = [psGU[:, 0:2, :].rearrange("p a b -> p (a b)"), psGU[:, 2:4, :].rearrange("p a b -> p (a b)")]
        bnd = [0, 6, 12, 17, NCH]
        win_v = w_in.rearrange("(c p) f -> p c f", p=128)
        for qi in range(4):
            for half in range(2):
                c0 = half * DFF + bnd[qi] * 128
                c1 = half * DFF + bnd[qi + 1] * 128
                for c in range(8):
                    rec.add("pool", "dma_start", out=win[:, c, c0:c1], in_=win_v[:, c, c0:c1],
                            w=[("win", c, half * 4 + qi)], dma=True, sem=f"win_{half}_{qi}")
        qof = [max(i for i in range(4) if bnd[i] <= j) for j in range(NCH)]
        load_w_rows(rec, wout, w_out, "wout", NCH, col_splits=1)

        ts.pre_block(0, 0)
        for b in range(nblk):
            slot = b % 2
            xnT = ts.xnT[slot]
            for j in range(NCH):
                pg, pu = psG[j % 2], psU[j % 2]
                for half, pt in ((0, pg), (1, pu)):
                    col = half * DFF + j * 128
                    q = half * 4 + qof[j]
                    for c in range(8):
                        rec.add("pe", "matmul", out=pt, lhsT=win[:, c, col:col + 128], rhs=xnT[:, c, :],
                                start=(c == 0), stop=(c == 7),
                                r=[("win", 7, q)] + (ts.xnT_keys(slot) if c == 0 else []),
                                w=[("psGU", half, j % 2)])
                g_ = gs[j % 2]
                rec.add("act", "activation", out=g_[:], in_=pg, func=AF.Silu,
                        r=[("psGU", 0, j % 2)], w=[("gs", j % 2)])
                rec.add("dve", "tensor_tensor", out=aT[:, j, :], in0=pu, in1=g_[:], op=ALU.mult,
                        r=[("psGU", 1, j % 2), ("gs", j % 2)], w=[("aT", j)])
                if b + 1 < nblk:
                    if j == 5:
                        phs = ts.pre_block_phases(b + 1, 1 - slot)
                        phs[0]()
                    elif j == 11:
                        phs[1]()
                    elif j == 16:
                        phs[2]()
            for t in range(tpb):
                g = b * tpb + t
                s = ts.post_load(g)
                if t % 2 == 0:
                    po, keys = psO[:], [("psO", 0), ("psO", 1)]
                else:
                    ai = t // 2 % 2
                    po, keys = alt[ai], [("psGU", ai, 0), ("psGU", ai, 1)]
                for hf in range(2):
                    for j in range(NCH):
                        rec.add("pe", "matmul", out=po[:, hf * 512:(hf + 1) * 512],
                                lhsT=aT[:, j, t * 128:(t + 1) * 128], rhs=wout[:, j, hf * 512:(hf + 1) * 512],
                                start=(j == 0), stop=(j == NCH - 1),
                                r=[("aT", j), ("wout", j, 0)], w=[keys[hf]])
                ts.post_tile(g, s, po, "psO", pkeys=keys)
        if dbg is not None:
            for nm, t in (("xnT", ts.xnT[0]), ("aT", aT), ("stat", ts.stat), ("tmp", ts.tmp), ("xsb", ts.xsb[0]),
                          ("gpre", ts.gpre), ("gpost", ts.gpost)):
                rec.add("sp", "dma_start", out=dbg[nm], in_=t[:], r=list(rec.last_w.keys()), dma=True, sem="dbg" + nm)
        rec.emit(nc, name)


def xattn_stage(nc, name, consts, src, dst, seqs, mem, w_q, w_kv, w_o, g_pre, g_mem, g_post):
    import contextlib
    with contextlib.ExitStack() as st:
        cx = Ctx(nc, name, st)
        rec = cx.rec
        ts = TokenStage(cx, consts, src, dst, g_pre, g_post, 1.0)
        tb, tpb = ts.tb, ts.tpb
        nseq = len(seqs)
        wq = cx.sb([128, 8, D], BF16, "wq")
        wkv = cx.sb([128, 8, 2 * D], BF16, "wkv")
        wo = cx.sb([128, 8, D], BF16, "wo")
        KT = [cx.sb([128, 8, N_MEM], BF16, "KT") for _ in range(nseq)]
        V = [cx.sb([128, 2, D], BF16, "V") for _ in range(nseq)]
        memT = cx.sb([128, 8, N_MEM], BF16, "memT")
        gmem = cx.sb([128, 8], F32, "gmem")
        qT = cx.sb([128, 8, tb], BF16, "qT")
        pT = cx.sb([128, 8, tb], BF16, "pT")
        oT = cx.sb([128, 8, tb], BF16, "oT")
        pe2 = [cx.sb([128, 4, N_MEM], BF16, "pexp") for _ in range(2)]
        pn2 = [cx.sb([128, 4, N_MEM], BF16, "pn") for _ in range(2)]
        sm2 = [cx.sb([128, 16], F32, "sm") for _ in range(2)]
        psO = cx.ps([128, D], F32, "psQO")
        psQ = [psO[:, 0:tb], psO[:, tb:2 * tb]]
        psS2 = [cx.ps([128, 4, N_MEM], F32, "psS") for _ in range(2)]
        psPT = ts.psT
        load_w_rows(rec, wkv, w_kv, "wkv", 8)
        load_w_rows(rec, wq, w_q, "wq", 8)
        load_w_rows(rec, wo, w_o, "wo", 8)
        rec.add("sp", "dma_start", out=gmem[:], in_=col_view(g_mem), allow_slow_non_contiguous=True,
                w=["gmem"], dma=True, sem="gmem")
        wkv_keys = [("wkv", c, 0) for c in range(8)]
        nq = 0
        for si in range(nseq):
            for mt in range(2):
                s = ts.pre_tile(si * 2 + mt, src=mem)
                for c in range(8):
                    rec.add("pe", "transpose", out=ts.psT[:, c, :], in_=ts.xsb[s][:, c * 128:(c + 1) * 128],
                            identity=ts.ident[:], r=[("xsb", s), "ident"], w=["psT"])
                rec.add("dve", "tensor_tensor", out=memT[:, :, mt * 128:(mt + 1) * 128], in0=ts.psT[:],
                        in1=gmem[:].unsqueeze(2).to_broadcast([128, 8, 128]), op=ALU.mult,
                        r=["psT", "gmem"], w=[("memT", mt)])
            for m in range(8):
                pq = psQ[nq % 2]
                for c in range(8):
                    rec.add("pe", "matmul", out=pq[:, 0:N_MEM], lhsT=wkv[:, c, m * 128:(m + 1) * 128], rhs=memT[:, c, :],
                            start=(c == 0), stop=(c == 7),
                            r=[("wkv", c, 0), ("memT", 0), ("memT", 1)], w=[("psQ", nq % 2)])
                rec.add("act" if m % 2 else "dve", "activation" if m % 2 else "tensor_copy",
                        **(dict(func=AF.Copy) if m % 2 else {}), out=KT[si][:, m, :], in_=pq[:, 0:N_MEM],
                        r=[("psQ", nq % 2)], w=[("KT", si)])
                nq += 1
            for mt in range(2):
                for hf in range(2):
                    pq = psQ[nq % 2]
                    for c in range(8):
                        rec.add("pe", "matmul", out=pq, lhsT=memT[:, c, mt * 128:(mt + 1) * 128],
                                rhs=wkv[:, c, D + hf * 512:D + (hf + 1) * 512], start=(c == 0), stop=(c == 7),
                                r=[("wkv", c, 0), ("memT", mt)], w=[("psQ", nq % 2)])
                    rec.add("act" if hf else "dve", "activation" if hf else "tensor_copy",
                            **(dict(func=AF.Copy) if hf else {}), out=V[si][:, mt, hf * 512:(hf + 1) * 512], in_=pq,
                            r=[("psQ", nq % 2)], w=[("V", si)])
                    nq += 1
        blocks = []
        for si, (t0, L) in enumerate(seqs):
            assert t0 % tb == 0 and L % tb == 0
            for b in range(t0 // tb, (t0 + L) // tb):
                blocks.append((b, si))
        ts.pre_block(blocks[0][0], 0)
        for bi, (b, si) in enumerate(blocks):
            slot = bi % 2
            xnT = ts.xnT[slot]
            for m in range(8):
                pq = psQ[nq % 2]
                for c in range(8):
                    rec.add("pe", "matmul", out=pq, lhsT=wq[:, c, m * 128:(m + 1) * 128], rhs=xnT[:, c, :],
                            start=(c == 0), stop=(c == 7),
                            r=[("wq", c, 0)] + (ts.xnT_keys(slot) if c == 0 else []), w=[("psQ", nq % 2)])
                rec.add("act" if m % 2 else "dve", "activation" if m % 2 else "tensor_copy",
                        **(dict(func=AF.Copy) if m % 2 else {}), out=qT[:, m, :], in_=pq,
                        r=[("psQ", nq % 2)], w=[("qT", m)])
                nq += 1

            def s_steps(t):
                tsl = slice(t * 128, (t + 1) * 128)
                u = t % 2
                psS, pe_, pn, sm = psS2[u], pe2[u], pn2[u], sm2[u]
                A = lambda *a, **kw: (a, kw)
                mx, nb, rsum, rinv = sm[:, 0:4], sm[:, 4:8], sm[:, 8:12], sm[:, 12:16]
                st_ = []
                mm = []
                for h in range(4):
                    for c2 in range(2):
                        mm.append(A("pe", "matmul", out=psS[:, h, :], lhsT=qT[:, 2 * h + c2, tsl], rhs=KT[si][:, 2 * h + c2, :],
                                    start=(c2 == 0), stop=(c2 == 1), r=[("qT", 2 * h + c2), ("KT", si)], w=[("psS", u)]))
                st_.append(mm)
                st_.append([A("dve", "tensor_reduce", out=mx, in_=psS[:], axis=AX.X, op=ALU.max, r=[("psS", u)], w=[("mx", u)])])
                st_.append([A("dve", "tensor_scalar", out=nb, in0=mx, scalar1=-1.0 / 16.0, scalar2=None, op0=ALU.mult,
                              r=[("mx", u)], w=[("nb", u)])])
                st_.append([A("act", "activation", out=pe_[:, h, :], in_=psS[:, h, :], func=AF.Exp, scale=1.0 / 16.0,
                              bias=nb[:, h:h + 1], accum_out=rsum[:, h:h + 1], r=[("psS", u), ("nb", u)],
                              w=[("pexp", u), ("rsum", u, h)]) for h in range(4)])
                st_.append([A("dve", "reciprocal", out=rinv, in_=rsum, r=[("rsum", u, h) for h in range(4)], w=[("rinv", u)])])
                st_.append([A("dve", "tensor_tensor", out=pn[:], in0=pe_[:], in1=rinv.unsqueeze(2).to_broadcast([128, 4, N_MEM]),
                              op=ALU.mult, r=[("pexp", u), ("rinv", u)], w=[("pn", u)])])
                return st_

            def emit_pt(t):
                tsl = slice(t * 128, (t + 1) * 128)
                u = t % 2
                pn = pn2[u]
                for h in range(4):
                    for mc in range(2):
                        rec.add("pe", "transpose", out=psPT[:, 2 * h + mc, :], in_=pn[:, h, mc * 128:(mc + 1) * 128],
                                identity=ts.ident[:], r=[("pn", u), "ident"], w=["psT"])
                rec.add("act", "activation", func=AF.Copy, out=pT[:, :, tsl], in_=psPT[:], r=["psT"], w=[("pT", t)])

            phs = ts.pre_block_phases(blocks[bi + 1][0], 1 - slot) if bi + 1 < len(blocks) else None
            if phs:
                phs[0]()
            for t0_ in range(0, tpb, 2):
                chains = [s_steps(t0_), s_steps(t0_ + 1)]
                for k_ in range(len(chains[0])):
                    for c_ in chains:
                        for (a_, kw_) in c_[k_]:
                            rec.add(*a_, **kw_)
                emit_pt(t0_)
                emit_pt(t0_ + 1)
                if phs:
                    phs[1 + t0_ // 2]()
            for h in range(4):
                for dc in range(2):
                    pq = psQ[nq % 2]
                    for mc in range(2):
                        rec.add("pe", "matmul", out=pq, lhsT=V[si][:, mc, h * 256 + dc * 128:h * 256 + (dc + 1) * 128],
                                rhs=pT[:, 2 * h + mc, :], start=(mc == 0), stop=(mc == 1),
                                r=[("V", si)] + [("pT", t) for t in range(tpb)], w=[("psQ", nq % 2)])
                    m = 2 * h + dc
                    rec.add("act" if m % 2 else "dve", "activation" if m % 2 else "tensor_copy",
                            **(dict(func=AF.Copy) if m % 2 else {}), out=oT[:, m, :], in_=pq,
                            r=[("psQ", nq % 2)], w=[("oT", m)])
                    nq += 1
            for t in range(tpb):
                g = b * tpb + t
                s = ts.post_load(g)
                if t in (1, 2):
                    u = t - 1
                    po, keys = psS2[u][:].rearrange("p a b -> p (a b)"), [("psS", u), ("psS", u)]
                else:
                    po, keys = psO[:], [("psQ", 0), ("psQ", 1)]
                for hf in range(2):
                    for c in range(8):
                        rec.add("pe", "matmul", out=po[:, hf * 512:(hf + 1) * 512], lhsT=oT[:, c, t * 128:(t + 1) * 128],
                                rhs=wo[:, c, hf * 512:(hf + 1) * 512], start=(c == 0), stop=(c == 7),
                                r=[("oT", c), ("wo", c, 0)], w=[keys[hf]])
                ts.post_tile(g, s, po, "psQ", pkeys=keys)
        rec.emit(nc, name)


HD = 8
CH = 64


def lb_prologue(nc, consts, lb_logits, depth):
    import contextlib
    with contextlib.ExitStack() as st:
        cx = Ctx(nc, "lbp", st)
        rec = cx.rec
        lg = cx.sb([128, depth, 8], F32, "lg")
        mx = cx.sb([128, 8], F32, "mx")
        ex = cx.sb([128, depth, 8], F32, "ex")
        sm = cx.sb([128, 8], F32, "sm")
        cs = cx.sb([128, 8], F32, "cs")
        lb, oml = consts["lb"], consts["oml"]
        rec.add("sp", "dma_start", out=lg[:], in_=lb_logits.rearrange("l (c p) -> p l c", p=128),
                allow_slow_non_contiguous=True, w=["lg"], dma=True, sem="lg")
        rec.add("dve", "tensor_copy", out=mx[:], in_=lg[:, 0, :], r=["lg"], w=["mx"])
        for l in range(1, depth):
            rec.add("dve", "tensor_tensor", out=mx[:], in0=mx[:], in1=lg[:, l, :], op=ALU.max, r=["lg", "mx"], w=["mx"])
        for l in range(depth):
            rec.add("dve", "tensor_tensor", out=ex[:, l, :], in0=lg[:, l, :], in1=mx[:], op=ALU.subtract,
                    r=["lg", "mx"], w=["ex"])
        rec.add("act", "activation", out=ex[:], in_=ex[:], func=AF.Exp, r=["ex"], w=["ex"])
        rec.add("dve", "tensor_copy", out=sm[:], in_=ex[:, 0, :], r=["ex"], w=["sm"])
        for l in range(1, depth):
            rec.add("dve", "tensor_tensor", out=sm[:], in0=sm[:], in1=ex[:, l, :], op=ALU.add, r=["ex", "sm"], w=["sm"])
        rec.add("dve", "reciprocal", out=sm[:], in_=sm[:], r=["sm"], w=["sm"])
        for l in range(depth):
            rec.add("dve", "tensor_tensor", out=ex[:, l, :], in0=ex[:, l, :], in1=sm[:], op=ALU.mult,
                    r=["ex", "sm"], w=["ex"])
        for l in range(depth):
            if l == 0:
                rec.add("dve", "tensor_copy", out=cs[:], in_=ex[:, 0, :], r=["ex"], w=["cs"])
            else:
                rec.add("dve", "tensor_tensor", out=cs[:], in0=cs[:], in1=ex[:, l, :], op=ALU.add, r=["ex", "cs"], w=["cs"])
            rec.add("dve", "tensor_tensor", out=lb[:, l, :], in0=cs[:], in1=ex[:, 0, :], op=ALU.subtract,
                    r=["cs", "ex"], w=["lb"])
            rec.add("dve", "tensor_scalar", out=lb[:, l, :], in0=lb[:, l, :], scalar1=0.0, scalar2=None, op0=ALU.max,
                    r=["lb"], w=["lb"])
            rec.add("dve", "tensor_scalar", out=oml[:, l, :], in0=lb[:, l, :], scalar1=-1.0, scalar2=1.0,
                    op0=ALU.mult, op1=ALU.add, r=["lb"], w=["oml"])
        rec.emit(nc, "lbp")


def h1_stage(nc, name, consts, src, nt, w_in, g_pre, gnorm, layer, scr):
    import contextlib
    with contextlib.ExitStack() as st:
        cx = Ctx(nc, name, st)
        rec = cx.rec
        ts = TokenStage(cx, consts, src, None, g_pre, None, 1.0, no_post=True)
        tb, tpb = ts.tb, ts.tpb
        nblk = nt // tb
        ncb = tb // CH
        w = cx.sb([128, 8, 5 * D], BF16, "w")
        load_w_rows(rec, w, w_in, "w", 8, col_splits=5)
        gnt = cx.sb([128, D], F32, "gnt")
        rec.add("sp", "dma_start", out=gnt[:].rearrange("p (h d) -> p h d", h=HD),
                in_=gnorm.partition_broadcast(128).unsqueeze(1).to_broadcast([128, HD, 128]),
                w=["gnt"], dma=True, sem="gnt")
        lbc, omlc = consts["lb"], consts["oml"]
        rmask = cx.sb([128, 512], F32, "rmask")
        rec.add("sp", "dma_start", out=rmask[:], in_=consts["d_rmask"], w=["rmask"], dma=True, sem="rmask")
        NZ = 7
        psZ = [cx.ps([128, tb], F32, "psZ") for _ in range(NZ)]
        qs2 = [cx.sb([128, tb], F32, "qs") for _ in range(2)]
        T = {}
        for d in range(2):
            for nm in ("f", "lc", "k", "bc", "ep", "em", "B", "e3"):
                T[(nm, d)] = [cx.sb([128, tb], F32, nm) for _ in range(2)]
            for nm in ("qd", "ke", "kd"):
                T[(nm, d)] = [cx.sb([128, tb], BF16, nm) for _ in range(2)]
            T[("bt", d)] = [cx.sb([128, ncb], F32, "bt") for _ in range(2)]
            T[("eb", d)] = [cx.sb([128, ncb], F32, "eb") for _ in range(2)]
        vst = [cx.sb([128, D], BF16, "vst") for _ in range(2)]
        gst = [cx.sb([128, D], BF16, "gst") for _ in range(2)]
        gsi = cx.sb([128, D], BF16, "gsi")
        ctr = dict(nz=0, nout=0)
        ts.pre_block(0, 0)
        for b in range(nblk):
            slot = b % 2
            xnT = ts.xnT[slot]
            tok = slice(b * tb, (b + 1) * tb)
            def head_chains(hd):
                hpar = hd % 2
                qs = qs2[hpar]
                pz = []
                for which in range(3):
                    p = psZ[ctr["nz"] % NZ]
                    pk = ("psZ", ctr["nz"] % NZ)
                    ctr["nz"] += 1
                    col = which * D + hd * 128
                    for c in range(8):
                        rec.add("pe", "matmul", out=p[:], lhsT=w[:, c, col:col + 128], rhs=xnT[:, c, :],
                                start=(c == 0), stop=(c == 7),
                                r=[("w", c, which)] + (ts.xnT_keys(slot) if c == 0 else []), w=[pk])
                    pz.append((p, pk))
                A = lambda *a, **kw: (a, kw)
                silu_step = [A("act", "activation", out=qs[:], in_=pz[0][0][:], func=AF.Silu, r=[pz[0][1]], w=[("qs", hpar)])]
                lb_h = lbc[:, layer, hd:hd + 1]
                oml_h = omlc[:, layer, hd:hd + 1]
                chains = []
                for d in range(2):
                    p, pk = pz[1 + d]
                    f, lc, k, bc, ep, em, B, e3 = (T[(nm, d)][hpar] for nm in ("f", "lc", "k", "bc", "ep", "em", "B", "e3"))
                    o = hpar
                    qd, ke, kd, eb = T[("qd", d)][o], T[("ke", d)][o], T[("kd", d)][o], T[("eb", d)][o]
                    bt = T[("bt", d)][hpar]
                    K = lambda nm, d=d, hpar=hpar: (nm, d, hpar)
                    KO = lambda nm, d=d, o=o: (nm, d, o)
                    bc3 = bc[:].rearrange("p (c t) -> p c t", t=CH)
                    lc3 = lc[:].rearrange("p (c t) -> p c t", t=CH)
                    B3 = B[:].rearrange("p (c t) -> p c t", t=CH)
                    btb = bt[:].unsqueeze(2).to_broadcast([128, ncb, CH])
                    st_ = []
                    st_.append([A("act", "activation", out=f[:], in_=p[:], func=AF.Sigmoid, r=[pk], w=[K("f")])])
                    st_.append([A("dve", "tensor_scalar", out=f[:], in0=f[:], scalar1=oml_h, scalar2=lb_h, op0=ALU.mult,
                                  op1=ALU.add, r=[K("f"), "lb", "oml"], w=[K("f")])])
                    st_.append([A("act", "activation", out=lc[:], in_=f[:], func=AF.Ln, r=[K("f")], w=[K("lc")]),
                                A("pool", "tensor_scalar", out=k[:], in0=f[:], scalar1=-1.0, scalar2=1.0, op0=ALU.mult,
                                  op1=ALU.add, r=[K("f")], w=[K("k")])])
                    st_.append([A("dve", "tensor_tensor_scan", out=bc[:], data0=rmask[:], data1=lc[:], initial=0.0,
                                  op0=ALU.mult, op1=ALU.add, r=[K("lc"), "rmask"], w=[K("bc")])])
                    st_.append([A("dve", "tensor_copy", out=bt[:], in_=bc3[:, :, CH - 1], r=[K("bc")], w=[K("bt")])])
                    st_.append([A("dve", "tensor_tensor", out=B3, in0=btb, in1=bc3, op=ALU.subtract, r=[K("bt"), K("bc")], w=[K("B")]),
                                A("act", "activation", out=eb[:], in_=bt[:], func=AF.Exp, r=[K("bt")], w=[KO("eb")])])
                    if d == 1:
                        st_.append([A("dve", "tensor_tensor", out=bc3, in0=B3, in1=lc3, op=ALU.add, r=[K("B"), K("lc")], w=[K("bc")])])
                        st_.append([A("dve", "tensor_tensor", out=B3, in0=btb, in1=bc3, op=ALU.subtract, r=[K("bt"), K("bc")], w=[K("B")])])
                    else:
                        st_.append([])
                        st_.append([])
                    st_.append([A("act", "activation", out=ep[:], in_=bc[:], func=AF.Exp, r=[K("bc")], w=[K("ep")])])
                    st_.append([A("act", "activation", out=em[:], in_=bc[:], func=AF.Exp, scale=-1.0, r=[K("bc")], w=[K("em")]),
                                A("pool", "tensor_tensor", out=qd[:], in0=qs[:], in1=ep[:], op=ALU.mult, r=[("qs", hpar), K("ep")], w=[KO("qd")])])
                    st_.append([A("act", "activation", out=e3[:], in_=B[:], func=AF.Exp, r=[K("B")], w=[K("e3")]),
                                A("pool", "tensor_tensor", out=ke[:], in0=k[:], in1=em[:], op=ALU.mult, r=[K("k"), K("em")], w=[KO("ke")])])
                    st_.append([A("pool", "tensor_tensor", out=kd[:], in0=k[:], in1=e3[:], op=ALU.mult, r=[K("k"), K("e3")], w=[KO("kd")])])
                    fin = []
                    for nm, tl in (("qd", qd), ("ke", ke), ("kd", kd)):
                        fin.append(A("sp", "dma_start", out=scr["h" + nm][d, hd, :, tok], in_=tl[:], r=[KO(nm)], dma=True,
                                     sem=f"o{nm}{d}{o}"))
                    fin.append(A("sp", "dma_start", out=scr["heb"][d, hd, :, b * ncb:(b + 1) * ncb], in_=eb[:], r=[KO("eb")],
                                 dma=True, sem=f"oeb{d}{o}"))
                    st_.append(fin)
                    chains.append(st_)
                chains[0][0] = silu_step + chains[0][0]
                return chains

            for hp_ in range(HD // 2):
                chains = head_chains(2 * hp_) + head_chains(2 * hp_ + 1)
                for k_ in range(max(len(c_) for c_ in chains)):
                    for c_ in chains:
                        if k_ < len(c_):
                            for (a_, kw_) in c_[k_]:
                                rec.add(*a_, **kw_)
            if b + 1 < nblk:
                ts.pre_block(b + 1, 1 - slot)
            for t in range(tpb):
                g = b * tpb + t
                rows = slice(g * 128, (g + 1) * 128)
                o = g % 2
                for which, stg, dstn in ((3, vst[o], "hv"), (4, gst[o], "hg")):
                    for hf in range(2):
                        p = psZ[ctr["nz"] % NZ]
                        pk = ("psZ", ctr["nz"] % NZ)
                        ctr["nz"] += 1
                        col = which * D + hf * 512
                        for c in range(8):
                            rec.add("pe", "matmul", out=p[:], lhsT=xnT[:, c, t * 128:(t + 1) * 128], rhs=w[:, c, col:col + 512],
                                    start=(c == 0), stop=(c == 7), r=[("w", c, which), ("xnT", slot, t)], w=[pk])
                        if which == 3:
                            rec.add("act", "activation", out=stg[:, hf * 512:(hf + 1) * 512], in_=p[:], func=AF.Copy,
                                    r=[pk], w=[("vst", o, hf)])
                        else:
                            rec.add("act", "activation", out=gsi[:, hf * 512:(hf + 1) * 512], in_=p[:], func=AF.Silu,
                                    r=[pk], w=[("gsi", hf)])
                            rec.add("pool", "tensor_tensor", out=stg[:, hf * 512:(hf + 1) * 512],
                                    in0=gsi[:, hf * 512:(hf + 1) * 512], in1=gnt[:, hf * 512:(hf + 1) * 512], op=ALU.mult,
                                    r=[("gsi", hf), "gnt"], w=[("gst", o, hf)])
                    kk = "vst" if which == 3 else "gst"
                    rec.add("sp", "dma_start", out=scr[dstn][rows, :], in_=stg[:], r=[(kk, o, 0), (kk, o, 1)], dma=True,
                            sem=f"o{kk}{o}")
        rec.emit(nc, name)


def h2_stage(nc, name, consts, seqs, scr):
    import contextlib
    with contextlib.ExitStack() as st:
        cx = Ctx(nc, name, st)
        rec = cx.rec
        tb = 512
        ncb = tb // CH
        ident = consts["ident"]
        trif, trib = consts["trif"], consts["trib"]
        QD = [cx.sb([128, HD, tb], BF16, "QD") for _ in range(2)]
        KE = [cx.sb([128, HD, tb], BF16, "KE") for _ in range(2)]
        KD = [cx.sb([128, HD, tb], BF16, "KD") for _ in range(2)]
        EB = [cx.sb([128, HD, ncb], F32, "EB") for _ in range(2)]
        VV = [cx.sb([CH, ncb, D], BF16, "VV") for _ in range(2)]
        OB = cx.sb([CH, ncb, D], F32, "OB")
        GG = cx.sb([CH, ncb, D], BF16, "GG")
        OG = cx.sb([CH, ncb, D], BF16, "OG")
        S = cx.sb([128, HD, 128], F32, "S")
        Sb = cx.sb([128, HD, 128], BF16, "Sb")
        sT = [cx.sb([CH, HD, CH], BF16, "sT") for _ in range(2)]
        kdt = [cx.sb([CH, HD, 128], BF16, "kdt") for _ in range(2)]
        osum = cx.sb([CH, D], F32, "osum")
        osq = cx.sb([CH, D], F32, "osq")
        st8 = cx.sb([CH, 16], F32, "st8")
        pss = cx.ps([CH, HD, CH], F32, "ps_s")
        psk = cx.ps([CH, HD, 128], BF16, "ps_k")
        pso = cx.ps([CH, D], F32, "ps_o")
        psu = cx.ps([128, HD, 128], F32, "ps_u")
        HH = HD // 2
        steps = []
        nld = 0
        for si, (t0, L) in enumerate(seqs):
            nb = L // tb
            for d in (1, 0):
                blks = list(range(nb)) if d == 0 else list(range(nb - 1, -1, -1))
                first = True
                for b in blks:
                    sl = nld % 2
                    nld += 1
                    chunks = list(range(ncb)) if d == 0 else list(range(ncb - 1, -1, -1))
                    for ci, c in enumerate(chunks):
                        steps.append(dict(d=d, T0=t0 + b * tb, sl=sl, c=c, first=first, load=(ci == 0), store=(ci == ncb - 1)))
                        first = False

        def loads(stp):
            d, T0, sl = stp["d"], stp["T0"], stp["sl"]
            tok = slice(T0, T0 + tb)
            for nm, tl in (("hqd", QD[sl]), ("hke", KE[sl]), ("hkd", KD[sl])):
                rec.add("sp", "dma_start", out=tl[:], in_=scr[nm][d, :, :, tok].rearrange("h p t -> p h t"),
                        w=[(nm, sl)], dma=True, sem=f"{nm}{sl}")
            rec.add("sp", "dma_start", out=EB[sl][:], in_=scr["heb"][d, :, :, T0 // CH:T0 // CH + ncb].rearrange("h p c -> p h c"),
                    w=[("heb", sl)], dma=True, sem=f"heb{sl}")
            rec.add("sp", "dma_start", out=VV[sl][:], in_=scr["hv"][tok, :].rearrange("(c p) f -> p c f", p=CH),
                    w=[("hv", sl)], dma=True, sem=f"hv{sl}")

        def loads_fwd(stp):
            d, T0 = stp["d"], stp["T0"]
            tok = slice(T0, T0 + tb)
            if d == 0:
                rec.add("sp", "dma_start", out=OB[:], in_=scr["hob"][tok, :].rearrange("(c p) f -> p c f", p=CH),
                        r=[("hobD", T0)], w=[("OB", c) for c in range(ncb)], dma=True, sem="OB")
                rec.add("sp", "dma_start", out=GG[:], in_=scr["hg"][tok, :].rearrange("(c p) f -> p c f", p=CH),
                        w=["GG"], dma=True, sem="GG")

        def stage_a(i):
            stp = steps[i]
            if stp["load"]:
                loads(stp)
            sl, c, d = stp["sl"], stp["c"], stp["d"]
            cs = slice(c * CH, (c + 1) * CH)
            i2 = i % 2
            tri = trif if d == 0 else trib
            for hd in range(HD):
                rec.add("pe", "matmul", out=pss[:, hd, :], lhsT=KE[sl][:, hd, cs], rhs=QD[sl][:, hd, cs], start=True, stop=True,
                        r=[("hke", sl), ("hqd", sl)], w=["ps_s"])
            for hd in range(HD):
                rec.add("pe", "transpose", out=psk[:, hd, :], in_=KD[sl][:, hd, cs], identity=ident[:],
                        r=[("hkd", sl), "ident"], w=["ps_k"])
            rec.add("dve", "tensor_tensor", out=sT[i2][:], in0=pss[:], in1=tri[:].unsqueeze(1).to_broadcast([CH, HD, CH]),
                    op=ALU.mult, r=["ps_s", "tri"], w=[("sT", i2)])
            rec.add("act", "activation", out=kdt[i2][:], in_=psk[:], func=AF.Copy, r=["ps_k"], w=[("kdt", i2)])

        def stage_b(i):
            stp = steps[i]
            sl, c, first = stp["sl"], stp["c"], stp["first"]
            i2 = i % 2
            for hd in range(HD):
                hs = slice(hd * 128, (hd + 1) * 128)
                rec.add("pe", "matmul", out=pso[:, hs], lhsT=sT[i2][:, hd, :], rhs=VV[sl][:, c, hs], start=(hd % HH == 0),
                        stop=first, skip_group_check=True, r=[("sT", i2), ("hv", sl)], w=[("ps_o", hd // HH)])
            for hd in range(HD):
                hs = slice(hd * 128, (hd + 1) * 128)
                rec.add("pe", "matmul", out=psu[:, hd, :], lhsT=kdt[i2][:, hd, :], rhs=VV[sl][:, c, hs], start=True, stop=True,
                        r=[("kdt", i2), ("hv", sl)], w=[("ps_u", hd // HH)])

        def stage_c(i):
            stp = steps[i]
            sl, c, d, first, T0 = stp["sl"], stp["c"], stp["d"], stp["first"], stp["T0"]
            cs = slice(c * CH, (c + 1) * CH)
            tok = slice(T0, T0 + tb)
            if not first:
                for hd in range(HD):
                    hs = slice(hd * 128, (hd + 1) * 128)
                    rec.add("pe", "matmul", out=pso[:, hs], lhsT=QD[sl][:, hd, cs], rhs=Sb[:, hd, :], start=False, stop=True,
                            skip_group_check=True, r=[("hqd", sl), ("Sb", hd // HH)], w=[("ps_o", hd // HH)])
            for hh in range(2):
                hsl = slice(hh * HH, (hh + 1) * HH)
                if first:
                    rec.add("dve", "tensor_copy", out=S[:, hsl, :], in_=psu[:, hsl, :], r=[("ps_u", hh)], w=[("S", hh)])
                else:
                    rec.add("pool", "tensor_tensor", out=S[:, hsl, :], in0=S[:, hsl, :],
                            in1=EB[sl][:, hsl, c:c + 1].to_broadcast([128, HH, 128]), op=ALU.mult,
                            r=[("S", hh), ("heb", sl)], w=[("S", hh)])
                    rec.add("dve", "tensor_tensor", out=S[:, hsl, :], in0=S[:, hsl, :], in1=psu[:, hsl, :], op=ALU.add,
                            r=[("S", hh), ("ps_u", hh)], w=[("S", hh)])
                rec.add("act", "activation", out=Sb[:, hsl, :], in_=S[:, hsl, :], func=AF.Copy, r=[("S", hh)], w=[("Sb", hh)])
            if d == 1:
                for hh in range(2):
                    fs = slice(hh * 512, (hh + 1) * 512)
                    rec.add("act", "activation", out=OB[:, c, fs], in_=pso[:, fs], func=AF.Copy, r=[("ps_o", hh)], w=[("OB", c)])
                if stp["store"]:
                    rec.add("sp", "dma_start", out=scr["hob"][tok, :].rearrange("(c p) f -> p c f", p=CH), in_=OB[:],
                            r=[("OB", c) for c in range(ncb)], w=[("hobD", T0)], dma=True, sem="oOUT")
            else:
                for hh in range(2):
                    fs = slice(hh * 512, (hh + 1) * 512)
                    rec.add("dve", "tensor_tensor", out=osum[:, fs], in0=pso[:, fs], in1=OB[:, c, fs], op=ALU.add,
                            r=[("ps_o", hh), ("OB", c)], w=["osum"])
                rec.add("pool", "tensor_tensor", out=osq[:], in0=osum[:], in1=osum[:], op=ALU.mult, r=["osum"], w=["osq"])
                ss = st8[:, 0:8]
                rs = st8[:, 8:16]
                rec.add("dve", "tensor_reduce", out=ss, in_=osq[:].rearrange("p (h d) -> p h d", h=HD), axis=AX.X,
                        op=ALU.add, r=["osq"], w=["ss"])
                rec.add("act", "activation", out=rs, in_=ss, func=AF.Sqrt, scale=1.0 / 128.0, bias=EPS, r=["ss"], w=["rs"])
                rec.add("dve", "reciprocal", out=rs, in_=rs, r=["rs"], w=["rs"])
                rec.add("pool", "tensor_tensor", out=osq[:].rearrange("p (h d) -> p h d", h=HD),
                        in0=osum[:].rearrange("p (h d) -> p h d", h=HD),
                        in1=rs.unsqueeze(2).to_broadcast([CH, HD, 128]), op=ALU.mult, r=["osum", "rs"], w=["osq"])
                rec.add("pool", "tensor_tensor", out=OG[:, c, :], in0=osq[:], in1=GG[:, c, :], op=ALU.mult,
                        r=["osq", "GG"], w=[("OG", c)])
                if stp["store"]:
                    rec.add("sp", "dma_start", out=scr["hog"][tok, :].rearrange("(c p) f -> p c f", p=CH), in_=OG[:],
                            r=[("OG", c) for c in range(ncb)], dma=True, sem="oOG")

        n = len(steps)
        loads_fwd(steps[0])
        stage_a(0)
        stage_b(0)
        for i in range(n):
            if i + 1 < n:
                stage_a(i + 1)
            stage_c(i)
            if i + 1 < n:
                if steps[i + 1]["load"]:
                    loads_fwd(steps[i + 1])
                stage_b(i + 1)
        rec.emit(nc, name)


def proj_out_stage(nc, name, consts, src, dst, nt, og, w_out, g_post):
    import contextlib
    with contextlib.ExitStack() as st:
        cx = Ctx(nc, name, st)
        rec = cx.rec
        ts = TokenStage(cx, consts, src, dst, None, g_post, 1.0, pre_from_bf16=True)
        tb, tpb = ts.tb, ts.tpb
        nblk = nt // tb
        wo = cx.sb([128, 8, D], BF16, "wo")
        load_w_rows(rec, wo, w_out, "wo", 8)
        psO = [cx.ps([128, D], F32, "psO") for _ in range(2)]
        ts.pre_block(0, 0, src=og)
        for b in range(nblk):
            slot = b % 2
            xnT = ts.xnT[slot]
            if b + 1 < nblk:
                ts.pre_block(b + 1, 1 - slot, src=og)
            for t in range(tpb):
                g = b * tpb + t
                s = ts.post_load(g)
                po = psO[g % 2]
                for hf in range(2):
                    for c in range(8):
                        rec.add("pe", "matmul", out=po[:, hf * 512:(hf + 1) * 512], lhsT=xnT[:, c, t * 128:(t + 1) * 128],
                                rhs=wo[:, c, hf * 512:(hf + 1) * 512], start=(c == 0), stop=(c == 7),
                                r=[("xnT", slot, t), ("wo", c, 0)], w=[("psO", g % 2, hf)])
                ts.post_tile(g, s, po[:], "psO", pkeys=[("psO", g % 2, 0), ("psO", g % 2, 1)])
        rec.emit(nc, name)


DILS = (1, 4, 16)
NH = 16
HDIM = 64
MASK_REL = 60000.0


def attn_const_arrays():
    k = np.arange(128)[:, None]
    q = np.arange(128)[None, :]
    relL = np.abs(q - k + 64).astype(np.float32)
    relR = np.abs(q - k - 64).astype(np.float32)
    L = np.where(relL <= 64, relL, MASK_REL)
    R = np.where(relR <= 64, relR, MASK_REL)
    L0 = np.where(k >= 64, L, MASK_REL)
    Rend = np.where(k < 64, R, MASK_REL)
    rel = np.stack([np.stack([L, R]), np.stack([L0, R]), np.stack([L, Rend]), np.stack([L0, Rend])]).astype(ml_dtypes.bfloat16)
    slopes = np.exp2(-8.0 * np.arange(1, NH + 1, dtype=np.float32) / NH)
    ci = np.zeros((len(DILS), NH, 128, 128), np.float32)
    for gi, d in enumerate(DILS):
        for h in range(NH):
            ci[gi, h] = -8.0 * slopes[h] * d * np.eye(128, dtype=np.float32)
    return rel, ci.astype(ml_dtypes.bfloat16)


def a1_stage(nc, name, consts, src, nt, w_in, g_pre, aqkv):
    import contextlib
    with contextlib.ExitStack() as st:
        cx = Ctx(nc, name, st)
        rec = cx.rec
        ts = TokenStage(cx, consts, src, None, g_pre, None, 1.0, no_post=True)
        tb, tpb = ts.tb, ts.tpb
        nblk = nt // tb
        w = cx.sb([128, 8, 9 * D], BF16, "w")
        load_w_rows(rec, w, w_in, "w", 8, col_splits=3)
        stg = [cx.sb([128, 3 * D], BF16, "stg") for _ in range(2)]
        psA = [cx.ps([128, 512], F32, "psA") for _ in range(4)]
        na = 0
        ns = 0
        ts.pre_block(0, 0)
        for b in range(nblk):
            slot = b % 2
            xnT = ts.xnT[slot]
            for t in range(tpb):
                g = b * tpb + t
                for gi in range(3):
                    so = ns % 2
                    ns += 1
                    for n in range(6):
                        col = gi * 3 * D + n * 512
                        p = psA[na % 4]
                        pk = ("psA", na % 4)
                        na += 1
                        for c in range(8):
                            rec.add("pe", "matmul", out=p[:], lhsT=xnT[:, c, t * 128:(t + 1) * 128], rhs=w[:, c, col:col + 512],
                                    start=(c == 0), stop=(c == 7), r=[("w", c, gi), ("xnT", slot, t)], w=[pk])
                        if n % 2:
                            rec.add("act", "activation", out=stg[so][:, n * 512:(n + 1) * 512], in_=p[:], func=AF.Copy,
                                    r=[pk], w=[("stg", so, n)])
                        else:
                            rec.add("dve", "tensor_copy", out=stg[so][:, n * 512:(n + 1) * 512], in_=p[:],
                                    r=[pk], w=[("stg", so, n)])
                    rec.add("sp", "dma_start", out=aqkv[g * 128:(g + 1) * 128, gi * 3 * D:(gi + 1) * 3 * D], in_=stg[so][:],
                            r=[("stg", so, n) for n in range(6)], dma=True, sem=f"ostg{so}")
                if b + 1 < nblk:
                    if t == 0:
                        phs = ts.pre_block_phases(b + 1, 1 - slot)
                        phs[0]()
                    elif t == 1:
                        phs[1]()
                    elif t == 2:
                        phs[2]()
        rec.emit(nc, name)


def attn_decay_table():
    k = np.arange(128)[:, None]
    q = np.arange(128)[None, :]
    rel = [np.abs(q - k + 64).astype(np.float64), np.abs(q - k - 64).astype(np.float64)]
    slopes = np.exp2(-8.0 * np.arange(1, NH + 1, dtype=np.float64) / NH)
    dm = np.zeros((128, len(DILS), NH // 2, 4, 128), np.float64)
    for gi, d in enumerate(DILS):
        for h in range(NH):
            for side in range(2):
                v = np.where(rel[side] <= 64, np.exp(-slopes[h] * d * rel[side]), 0.0)
                dm[:, gi, h // 2, (h % 2) * 2 + side, :] = v
    return dm.astype(np.float32).astype(ml_dtypes.bfloat16)


HPB = 6


def a2_stage(nc, name, consts, seqs, aqkv, ao, al):
    import contextlib
    with contextlib.ExitStack() as st:
        cx = Ctx(nc, name, st)
        rec = cx.rec
        ident = consts["ident"]
        dm = cx.sb([128, len(DILS), NH // 2, 4, 128], BF16, "dm")
        rec.add("sp", "dma_start", out=dm[:], in_=consts["d_dm"], w=["dm"], dma=True, sem="dm")
        NKS = 6
        NKR = 4
        NQ = 3
        KR = [cx.sb([128, D], BF16, "KR") for _ in range(NKR)]
        QR = [cx.sb([128, D], BF16, "QR") for _ in range(NQ)]
        VA = [cx.sb([128, NH, HDIM + 1], BF16, "VA") for _ in range(NKS)]
        kT = [cx.sb([128, 8, 128], BF16, "kT") for _ in range(NKS)]
        qT = [cx.sb([128, 8, 128], BF16, "qT") for _ in range(NQ)]
        ET = [cx.sb([128, 2, 4, 128], BF16, "ET") for _ in range(2)]
        PT = [cx.sb([128, 2, 4, 128], BF16, "PT") for _ in range(3)]
        OS = [cx.sb([128, D], F32, "OS") for _ in range(2)]
        DS = [cx.sb([128, NH], F32, "DS") for _ in range(2)]
        psTK = cx.ps([128, 8, 128], BF16, "psTK")
        psTQ = psTK
        psS = [cx.ps([128, 2, 512], F32, "psS") for _ in range(2)]
        psO = [cx.ps([128, 512], F32, "psO") for _ in range(3)]
        cnt = dict(kr=0, qr=0, ks=0)
        kstate = {}

        def load_key(gi, t0, n, d, r, j):
            key = (gi, t0, r, j)
            if key in kstate:
                return key
            sl = cnt["ks"] % NKS
            cnt["ks"] += 1
            for k_ in [k_ for k_, v_ in kstate.items() if v_["sl"] == sl]:
                del kstate[k_]
            kr = cnt["kr"] % NKR
            cnt["kr"] += 1
            kstate[key] = dict(sl=sl, kr=kr, ready=False)
            lo = max(0, 128 * j - 64)
            hi = min(n, 128 * j + 64)
            p0 = lo - (128 * j - 64)
            p1 = hi - (128 * j - 64)
            tok0 = t0 + lo * d + r
            rows = slice(tok0, tok0 + (hi - lo - 1) * d + 1, d)
            colk = gi * 3 * D + D
            colv = gi * 3 * D + 2 * D
            if p0 > 0:
                rec.add("pool", "memset", ap=KR[kr][0:p0, :], constant=0.0, w=[("KR", kr)])
                rec.add("pool", "memset", ap=VA[sl][0:p0, :, :], constant=0.0, w=[("VA", sl)])
            if p1 < 128:
                rec.add("pool", "memset", ap=KR[kr][p1:128, :], constant=0.0, w=[("KR", kr)])
                rec.add("pool", "memset", ap=VA[sl][p1:128, :, :], constant=0.0, w=[("VA", sl)])
            rec.add("pool", "memset", ap=VA[sl][p0:p1, :, HDIM:HDIM + 1], constant=1.0, w=[("VA", sl)])
            rec.add("sp", "dma_start", out=KR[kr][p0:p1, :], in_=aqkv[rows, colk:colk + D], r=[("KR", kr)], w=[("KR", kr)],
                    dma=True, sem=f"KR{kr}")
            rec.add("sp", "dma_start", out=VA[sl][p0:p1, :, 0:HDIM],
                    in_=aqkv[rows, colv:colv + D].rearrange("r (h d) -> r h d", h=NH), r=[("VA", sl)], w=[("VA", sl)],
                    dma=True, sem=f"VA{sl}")
            return key

        def trans_key(key):
            st_ = kstate[key]
            sl, kr = st_["sl"], st_["kr"]
            if not st_["ready"]:
                for c in range(8):
                    rec.add("pe", "transpose", out=psTK[:, c, :], in_=KR[kr][:, c * 128:(c + 1) * 128], identity=ident[:],
                            r=[("KR", kr), "ident"], w=["psTK"])
                rec.add("dve", "tensor_copy", out=kT[sl][:], in_=psTK[:], r=["psTK"], w=[("kT", sl)])
                st_["ready"] = True
            return sl

        def prep_loads(blk):
            gi, t0, n, d, r, qb, nb = blk
            kl = load_key(gi, t0, n, d, r, qb)
            kr_ = load_key(gi, t0, n, d, r, qb + 1)
            s = cnt["qr"] % NQ
            cnt["qr"] += 1
            tok0 = t0 + 128 * qb * d + r
            rows = slice(tok0, tok0 + 127 * d + 1, d)
            colq = gi * 3 * D
            rec.add("sp", "dma_start", out=QR[s][:], in_=aqkv[rows, colq:colq + D], w=[("QR", s)], dma=True, sem=f"QR{s}")
            return dict(keys=(kl, kr_), qs=s, rows=rows, gi=gi)

        def prep_trans(info):
            info["sl"] = (trans_key(info["keys"][0]), trans_key(info["keys"][1]))
            s = info["qs"]
            for c in range(8):
                rec.add("pe", "transpose", out=psTQ[:, c, :], in_=QR[s][:, c * 128:(c + 1) * 128], identity=ident[:],
                        r=[("QR", s), "ident"], w=["psTK"])
            rec.add("act", "activation", out=qT[s][:], in_=psTQ[:], func=AF.Copy, r=["psTK"], w=[("qT", s)])

        blocks = []
        for gi, d in enumerate(DILS):
            for (t0, L) in seqs:
                n = L // d
                assert n % 128 == 0
                nb = n // 128
                for r in range(d):
                    for qb in range(nb):
                        blocks.append((gi, t0, n, d, r, qb, nb))

        NP = NH // 4

        def emit_s(info, bi, up, u):
            pb, pt, eb = u % 2, u % 3, u % 2
            for pr in (0, 1):
                hp = 2 * up + pr
                for side in (0, 1):
                    for hh in (0, 1):
                        base = hh * 64
                        ksl = info["sl"][side]
                        c0 = pr * 256 + side * 128
                        rec.add("pe", "matmul", out=psS[pb][:, hh, c0:c0 + 128], lhsT=kT[ksl][base:base + 64, hp, :],
                                rhs=qT[info["qs"]][base:base + 64, hp, :], start=True, stop=True,
                                r=[("kT", ksl), ("qT", info["qs"])], w=[("psS", pb)])
            rec.add("act", "activation", out=ET[eb][:].rearrange("p a b q -> p a (b q)"), in_=psS[pb][:],
                    func=AF.Exp, scale=0.125, r=[("psS", pb)], w=[("ET", eb)])
            for hh in (0, 1):
                dmv = dm[:, info["gi"], 2 * up:2 * up + 2, hh * 2:hh * 2 + 2, :]
                rec.add("dve", "tensor_tensor", out=PT[pt][:, hh].rearrange("p (r s) q -> p r s q", r=2),
                        in0=ET[eb][:, hh].rearrange("p (r s) q -> p r s q", r=2), in1=dmv, op=ALU.mult,
                        r=[("ET", eb), "dm"], w=[("PT", pt, hh)])

        def emit_pv(info, bi, up, u):
            pt = u % 3
            for pr in (0, 1):
                hp = 2 * up + pr
                for hh in (0, 1):
                    h = 2 * hp + hh
                    bk, hi_ = h // HPB, h % HPB
                    for side in (0, 1):
                        ksl = info["sl"][side]
                        rec.add("pe", "matmul", out=psO[bk][:, hi_ * (HDIM + 1):(hi_ + 1) * (HDIM + 1)],
                                lhsT=PT[pt][:, hh, pr * 2 + side, :], rhs=VA[ksl][:, h, :], start=(side == 0), stop=(side == 1),
                                r=[("PT", pt, hh), ("VA", ksl)], w=[("psO", bk)])
                self_evac(info, bi, hp)

        def self_evac(info, bi, hp):
            h_last = 2 * hp + 1
            if h_last % HPB == HPB - 1 or h_last == NH - 1:
                bk = h_last // HPB
                h0 = bk * HPB
                nh = h_last - h0 + 1
                o = bi % 2
                src = psO[bk][:, 0:nh * (HDIM + 1)].rearrange("p (h e) -> p h e", e=HDIM + 1)
                rec.add("dve", "tensor_copy", out=OS[o][:, h0 * HDIM:(h0 + nh) * HDIM].rearrange("p (h d) -> p h d", d=HDIM),
                        in_=src[:, :, 0:HDIM], r=[("psO", bk)], w=[("OS", o, bk)])
                rec.add("dve", "tensor_copy", out=DS[o][:, h0:h0 + nh], in_=src[:, :, HDIM], r=[("psO", bk)], w=[("DS", o, bk)])
                if h_last == NH - 1:
                    gi = info["gi"]
                    rows = info["rows"]
                    nbk = (NH + HPB - 1) // HPB
                    rec.add("sp", "dma_start", out=ao[gi][rows, :], in_=OS[o][:], r=[("OS", o, b_) for b_ in range(nbk)], dma=True,
                            sem=f"oOS{o}")
                    rec.add("sp", "dma_start", out=al[gi][rows, :], in_=DS[o][:], r=[("DS", o, b_) for b_ in range(nbk)], dma=True,
                            sem=f"oDS{o}")

        infos = {0: prep_loads(blocks[0])}
        prep_trans(infos[0])
        if len(blocks) > 1:
            infos[1] = prep_loads(blocks[1])
        pending = None
        u = 0
        for bi, blk in enumerate(blocks):
            info = infos[bi]
            for hp in range(NP):
                emit_s(info, bi, hp, u)
                if pending is not None:
                    emit_pv(*pending)
                pending = (info, bi, hp, u)
                u += 1
                if hp == 0 and bi + 2 < len(blocks):
                    infos[bi + 2] = prep_loads(blocks[bi + 2])
                if hp == 1 and bi + 1 < len(blocks):
                    prep_trans(infos[bi + 1])
            infos.pop(bi - 1, None)
        emit_pv(*pending)
        rec.emit(nc, name)


def a3_stage(nc, name, consts, src, dst, nt, ao, al, w_out, g_post):
    import contextlib
    with contextlib.ExitStack() as st:
        cx = Ctx(nc, name, st)
        rec = cx.rec
        ts = TokenStage(cx, consts, src, dst, None, g_post, 1.0, pre_from_bf16=True)
        tb, tpb = ts.tb, ts.tpb
        nblk = nt // tb
        wo = cx.sb([128, 8, D], BF16, "wo")
        load_w_rows(rec, wo, w_out, "wo", 8)
        psO2 = [cx.ps([128, D], F32, "psO") for _ in range(2)]
        O = [[cx.sb([128, D], F32, "O") for _ in range(3)] for _ in range(2)]
        Lx = [[cx.sb([128, NH], F32, "L") for _ in range(3)] for _ in range(2)]
        npre = [0]

        def pre_block(b, slot):
            for t in range(tpb):
                g = b * tpb + t
                i = npre[0]
                npre[0] += 1
                s = i % 2
                rows = slice(g * 128, (g + 1) * 128)
                for gi in range(3):
                    rec.add("sp", "dma_start", out=O[s][gi][:], in_=ao[gi][rows, :], w=[("O", s, gi)], dma=True, sem=f"O{s}{gi}")
                    rec.add("sp", "dma_start", out=Lx[s][gi][:], in_=al[gi][rows, :], w=[("L", s, gi)], dma=True, sem=f"L{s}{gi}")
                for gi in (1, 2):
                    rec.add("dve", "tensor_tensor", out=O[s][0][:], in0=O[s][0][:], in1=O[s][gi][:], op=ALU.add,
                            r=[("O", s, 0), ("O", s, gi)], w=[("O", s, 0)])
                    rec.add("dve", "tensor_tensor", out=Lx[s][0][:], in0=Lx[s][0][:], in1=Lx[s][gi][:], op=ALU.add,
                            r=[("L", s, 0), ("L", s, gi)], w=[("L", s, 0)])
                rec.add("dve", "reciprocal", out=Lx[s][0][:], in_=Lx[s][0][:], r=[("L", s, 0)], w=[("L", s, 0)])
                rec.add("dve", "tensor_tensor", out=ts.xsb[s][:].rearrange("p (h d) -> p h d", h=NH),
                        in0=O[s][0][:].rearrange("p (h d) -> p h d", h=NH),
                        in1=Lx[s][0][:].unsqueeze(2).to_broadcast([128, NH, HDIM]), op=ALU.mult,
                        r=[("O", s, 0), ("L", s, 0)], w=[("xsb", s)])
                ts.transpose_tile(s, slot, t)

        pre_block(0, 0)
        for b in range(nblk):
            slot = b % 2
            xnT = ts.xnT[slot]
            if b + 1 < nblk:
                pre_block(b + 1, 1 - slot)
            for t in range(tpb):
                g = b * tpb + t
                s = ts.post_load(g)
                psO = psO2[g % 2]
                for hf in range(2):
                    for c in range(8):
                        rec.add("pe", "matmul", out=psO[:, hf * 512:(hf + 1) * 512], lhsT=xnT[:, c, t * 128:(t + 1) * 128],
                                rhs=wo[:, c, hf * 512:(hf + 1) * 512], start=(c == 0), stop=(c == 7),
                                r=[("xnT", slot, t), ("wo", c, 0)], w=[("psO", g % 2, hf)])
                ts.post_tile(g, s, psO[:], "psO", pkeys=[("psO", g % 2, 0), ("psO", g % 2, 1)])
        rec.emit(nc, name)


def const_arrays():
    rm = np.ones((128, 512), np.float32)
    rm[:, ::CH] = 0.0
    s = np.arange(CH)
    return {
        "c_ident": np.eye(128).astype(ml_dtypes.bfloat16),
        "c_rmask": rm,
        "c_trif": (s[:, None] <= s[None, :]).astype(ml_dtypes.bfloat16),
        "c_trib": (s[:, None] >= s[None, :]).astype(ml_dtypes.bfloat16),
        "c_dm": attn_decay_table(),
        "c_ones": np.ones((128, 1), ml_dtypes.bfloat16),
    }


CONST_SPECS = (("ident", [128, 128], BF16), ("rmask", [128, 512], F32), ("trif", [CH, CH], BF16), ("trib", [CH, CH], BF16),
               ("dm", [128, 3, NH // 2, 4, 128], BF16), ("ones", [128, 1], BF16))


def build_program(seqs, depth=4, n_layers_w=None, stop_after=None):
    import contextlib
    nt = sum(L for _, L in seqs)
    nseq = len(seqs)
    nw = depth if n_layers_w is None else n_layers_w
    na = (nw + 1) // 2
    nb_ = nw // 2
    nc = bass.Bass("TRN2", target_bir_lowering=False)
    di = lambda name, shape, dt=F32: nc.dram_tensor(name, list(shape), dt, kind="ExternalInput").ap()
    x = di("x", [nt, D])
    mem = di("mem", [nseq * N_MEM, D])
    norm_gains = di("norm_gains", [nw, 9, D])
    ffn_w_in = di("ffn_w_in", [nw, 2, D, 2 * DFF])
    ffn_w_out = di("ffn_w_out", [nw, 2, DFF, D])
    hg_w_in = di("hg_w_in", [na, D, 5 * D])
    hg_lb_logits = di("hg_lb_logits", [nw, D])
    hg_gnorm = di("hg_gnorm", [na, 128])
    hg_w_out = di("hg_w_out", [na, D, D])
    att_w_in = di("att_w_in", [max(nb_, 1), D, 9 * D])
    att_w_out = di("att_w_out", [max(nb_, 1), D, D])
    xa_w_q = di("xa_w_q", [nw, D, D])
    xa_w_kv = di("xa_w_kv", [nw, D, 2 * D])
    xa_w_o = di("xa_w_o", [nw, D, D])
    cds = {nm: di("c_" + nm, shp, dt_) for nm, shp, dt_ in CONST_SPECS}
    y = nc.dram_tensor("y", [nt, D], F32, kind="ExternalOutput").ap()
    ds = lambda name, shape, dt=F32: nc.dram_tensor(name, list(shape), dt, kind="Internal").ap()
    xs = ds("xs", [nt, D])
    scr = {}
    for nm in ("hqd", "hke", "hkd"):
        scr[nm] = ds(nm, [2, HD, 128, nt], BF16)
    scr["heb"] = ds("heb", [2, HD, 128, nt // CH])
    scr["hv"] = ds("hv", [nt, D], BF16)
    scr["hg"] = ds("hg", [nt, D], BF16)
    scr["hob"] = ds("hob", [nt, D])
    scr["hog"] = ds("hog", [nt, D], BF16)
    aqkv = ds("aqkv", [nt, 9 * D], BF16)
    ao = [ds(f"ao{i}", [nt, D]) for i in range(3)]
    al = [ds(f"al{i}", [nt, NH]) for i in range(3)]
    with contextlib.ExitStack() as st:
        consts = {}
        r = Rec()
        for nm, shp, dt_ in CONST_SPECS:
            if nm in ("dm", "rmask"):
                consts["d_" + nm] = cds[nm]
                continue
            consts[nm] = st.enter_context(nc.sbuf_tensor("k_" + nm, shp, dt_))
            r.add("sp", "dma_start", out=consts[nm][:], in_=cds[nm], w=[nm], dma=True, sem=nm)
        consts["lb"] = st.enter_context(nc.sbuf_tensor("k_lb", [128, nw, 8], F32))
        consts["oml"] = st.enter_context(nc.sbuf_tensor("k_oml", [128, nw, 8], F32))
        r.emit(nc, "pro")
        lb_prologue(nc, consts, hg_lb_logits, nw)
        stages = []
        for i in range(depth):
            g = norm_gains[i]
            j = i // 2
            stages.append(("ffn", i, 0))
            stages.append(("mix", i, 0))
            stages.append(("xa", i, 0))
            stages.append(("ffn", i, 1))
        if stop_after is not None:
            stages = stages[:stop_after]
        cur = x
        for si, (kind, i, k) in enumerate(stages):
            g = norm_gains[i]
            j = i // 2
            last = si == len(stages) - 1
            dst = y if last else xs
            nm = f"L{i}{kind}{k}"
            if kind == "ffn":
                ffn_stage(nc, nm, consts, cur, dst, nt, ffn_w_in[i, k], ffn_w_out[i, k], g[0 if k == 0 else 7], g[1 if k == 0 else 8])
            elif kind == "mix" and i % 2 == 0:
                h1_stage(nc, nm + "a", consts, cur, nt, hg_w_in[j], g[2], hg_gnorm[j], i, scr)
                h2_stage(nc, nm + "b", consts, seqs, scr)
                proj_out_stage(nc, nm + "c", consts, cur, dst, nt, scr["hog"], hg_w_out[j], g[3])
            elif kind == "mix":
                a1_stage(nc, nm + "a", consts, cur, nt, att_w_in[j], g[2], aqkv)
                a2_stage(nc, nm + "b", consts, seqs, aqkv, ao, al)
                a3_stage(nc, nm + "c", consts, cur, dst, nt, ao, al, att_w_out[j], g[3])
            else:
                xattn_stage(nc, nm, consts, cur, dst, seqs, mem, xa_w_q[i], xa_w_kv[i], xa_w_o[i], g[4], g[5], g[6])
            cur = dst
    return nc


N_CORES = 8
P_B, P_L = 16, 2048
S_B, S_L = 16, 4096
_PROG = {}


def kernel(x_prompt, x_sample, mem_prompt, mem_sample, norm_gains, ffn_w_in, ffn_w_out, hg_w_in, hg_lb_logits,
           hg_gnorm, hg_w_out, att_w_in, att_w_out, xa_w_q, xa_w_kv, xa_w_o):
    f = lambda a: np.ascontiguousarray(np.asarray(a, dtype=np.float32))
    x_prompt, x_sample, mem_prompt, mem_sample = f(x_prompt), f(x_sample), f(mem_prompt), f(mem_sample)
    ppc, spc = P_B // N_CORES, S_B // N_CORES
    seqs = []
    t = 0
    for _ in range(ppc):
        seqs.append((t, P_L))
        t += P_L
    for _ in range(spc):
        seqs.append((t, S_L))
        t += S_L
    if "nc" not in _PROG:
        _PROG["nc"] = build_program(seqs)
    nc = _PROG["nc"]
    shared = dict(norm_gains=f(norm_gains), ffn_w_in=f(ffn_w_in), ffn_w_out=f(ffn_w_out), hg_w_in=f(hg_w_in),
                  hg_lb_logits=f(hg_lb_logits), hg_gnorm=f(hg_gnorm), hg_w_out=f(hg_w_out), att_w_in=f(att_w_in),
                  att_w_out=f(att_w_out), xa_w_q=f(xa_w_q), xa_w_kv=f(xa_w_kv), xa_w_o=f(xa_w_o))
    shared.update(const_arrays())
    in_maps = []
    for c in range(N_CORES):
        xp = x_prompt[c * ppc:(c + 1) * ppc].reshape(-1, D)
        xsm = x_sample[c * spc:(c + 1) * spc].reshape(-1, D)
        mp = mem_prompt[c * ppc:(c + 1) * ppc].reshape(-1, D)
        ms = mem_sample[c * spc:(c + 1) * spc].reshape(-1, D)
        m = dict(shared)
        m["x"] = np.ascontiguousarray(np.concatenate([xp, xsm], axis=0))
        m["mem"] = np.ascontiguousarray(np.concatenate([mp, ms], axis=0))
        in_maps.append(m)
    res = run_bass_kernel_spmd(nc, in_maps, core_ids=list(range(N_CORES)))
    ys = [np.asarray(r["y"]) for r in res.results]
    npp = ppc * P_L
    y_prompt = np.stack([y[:npp].reshape(ppc, P_L, D) for y in ys]).reshape(P_B, P_L, D)
    y_sample = np.stack([y[npp:].reshape(spc, S_L, D) for y in ys]).reshape(S_B, S_L, D)
    return (y_prompt.astype(np.float32), y_sample.astype(np.float32))
```

```python
import numpy as np
import ml_dtypes
import concourse.bass as bass
import concourse.mybir as mybir
from concourse.bass_utils import run_bass_kernel_spmd

F32 = mybir.dt.float32
BF16 = mybir.dt.bfloat16
AF = mybir.ActivationFunctionType
ALU = mybir.AluOpType
AX = mybir.AxisListType

D = 1024
DFF = 2816
NCH = DFF // 128
EPS = 1e-6
N_MEM = 256
ENGS = ("pe", "act", "dve", "pool", "sp")
EPOCH = 30000


class Op:
    __slots__ = ("eng", "fn", "dma", "sem", "deps", "idx", "signal", "sig_no", "waits", "dval")

    def __init__(self, eng, fn, dma, sem):
        self.eng = eng
        self.fn = fn
        self.dma = dma
        self.sem = sem
        self.deps = ()
        self.signal = False
        self.sig_no = None
        self.waits = None
        self.dval = None


class Rec:
    def __init__(self):
        self.ops = {e: [] for e in ENGS}
        self.last_w = {}
        self.readers = {}
        self.dma_count = {}

    def add(self, eng, meth, r=(), w=(), dma=False, sem=None, **kw):
        op = Op(eng, (meth, kw), dma, sem)
        deps = []
        for k in r:
            lw = self.last_w.get(k)
            if lw is not None:
                deps.append(lw)
        for k in w:
            lw = self.last_w.get(k)
            if lw is not None:
                deps.append(lw)
            rd = self.readers.get(k)
            if rd:
                for v in rd.values():
                    if isinstance(v, list):
                        deps.extend(v)
                    else:
                        deps.append(v)
        op.deps = deps
        op.idx = len(self.ops[eng])
        self.ops[eng].append(op)
        if dma:
            assert sem is not None
            c = self.dma_count.get(sem, 0) + 1
            self.dma_count[sem] = c
            op.dval = 16 * c
        for k in w:
            self.last_w[k] = op
            self.readers[k] = {}
        for k in r:
            d = self.readers.setdefault(k, {})
            if dma:
                d.setdefault(("dma", eng), []).append(op)
            else:
                d[eng] = op
        return op

    def emit(self, nc, name):
        for e in ENGS:
            seen = {}
            seen_dma = {}
            for op in self.ops[e]:
                waits_c = {}
                waits_d = {}
                for d in op.deps:
                    if d is op:
                        continue
                    if d.dma:
                        if seen_dma.get(d.sem, 0) >= d.dval:
                            continue
                        if waits_d.get(d.sem, 0) < d.dval:
                            waits_d[d.sem] = d.dval
                    else:
                        if d.eng == e:
                            if e == "pe" or e == "sp":
                                continue
                            if d.idx >= op.idx:
                                continue
                        if seen.get(d.eng, -1) >= d.idx:
                            continue
                        if waits_c.get(d.eng, (-1, None))[0] < d.idx:
                            waits_c[d.eng] = (d.idx, d)
                for pe_, (i, d) in waits_c.items():
                    seen[pe_] = i
                    d.signal = True
                for s, v in waits_d.items():
                    seen_dma[s] = v
                op.waits = (waits_c, waits_d)
        nsig = {}
        for e in ENGS:
            n = 0
            for op in self.ops[e]:
                if op.signal and not op.dma:
                    op.sig_no = n
                    n += 1
            nsig[e] = n
        allsems = []

        def mk(nm):
            h = nc.alloc_semaphore(nm)
            allsems.append(h)
            return h

        if True:
            csem = {}
            for e in ENGS:
                csem[e] = [mk(f"{name}_c_{e}_{i}") for i in range((nsig[e] + EPOCH - 1) // EPOCH)]
            dsem = {s: mk(f"{name}_d_{s}") for s in self.dma_count}
        with nc.Block() as block:

            def run(e, eng):
                for op in self.ops[e]:
                    wc, wd = op.waits
                    for pe_, (i, d) in wc.items():
                        eng.wait_ge(csem[pe_][d.sig_no // EPOCH], d.sig_no % EPOCH + 1)
                    for s, v in wd.items():
                        eng.wait_ge(dsem[s], v)
                    ins = getattr(eng, op.fn[0])(**op.fn[1])
                    if op.dma:
                        ins.then_inc(dsem[op.sem], 16)
                    elif op.signal:
                        ins.then_inc(csem[e][op.sig_no // EPOCH], 1)
                mine = {}
                for op in self.ops[e]:
                    if op.dma:
                        mine[op.sem] = max(mine.get(op.sem, 0), op.dval)
                for s, v in mine.items():
                    eng.wait_ge(dsem[s], v)

            if self.ops["pe"]:
                block.tensor(lambda eng: run("pe", eng))
            if self.ops["act"]:
                block.scalar(lambda eng: run("act", eng))
            if self.ops["dve"]:
                block.vector(lambda eng: run("dve", eng))
            if self.ops["pool"]:
                block.gpsimd(lambda eng: run("pool", eng))
            if self.ops["sp"]:
                block.sync(lambda eng: run("sp", eng))
        nc.all_engine_barrier()
        nc.clear_and_free_semaphores(allsems)
        nc.all_engine_barrier()


class Ctx:
    def __init__(self, nc, name, stack):
        self.nc = nc
        self.name = name
        self.st = stack
        self.rec = Rec()
        self.n = 0

    def sb(self, shape, dt, name=None):
        self.n += 1
        t = self.st.enter_context(self.nc.sbuf_tensor(f"{self.name}_{name or 't'}{self.n}", list(shape), dt))
        return t

    def ps(self, shape, dt=F32, name=None):
        self.n += 1
        return self.st.enter_context(self.nc.psum_tensor(f"{self.name}_{name or 'p'}{self.n}", list(shape), dt))


def col_view(vec_ap):
    return vec_ap.rearrange("(c p) -> p c", p=128)


class TokenStage:
    def __init__(self, cx, consts, src, dst, g_pre_ap, g_post_ap, out_scale, tb=512, pre_from_bf16=False, no_post=False):
        self.cx = cx
        self.rec = cx.rec
        self.src = src
        self.dst = dst
        self.tb = tb
        self.tpb = tb // 128
        self.out_scale = out_scale
        self.ident = consts["ident"]
        rec = self.rec
        self.pre_from_bf16 = pre_from_bf16
        if not pre_from_bf16:
            self.xpre = [cx.sb([128, D], F32, "xpre") for _ in range(2)]
        if not no_post:
            self.xpost = [cx.sb([128, D], F32, "xpost") for _ in range(2)]
            self.tmp = cx.sb([128, D], F32, "tmp")
            self.gpost = cx.sb([128, D], F32, "gpost")
        self.xsb = [cx.sb([128, D], BF16, "xsb") for _ in range(2)]
        self.xnT = [cx.sb([128, 8, tb], BF16, "xnT") for _ in range(2)]
        self.sq = cx.sb([128, D], BF16, "sqjunk")
        self.stat = cx.sb([128, 16], F32, "stat")
        self.gpre = cx.sb([128, 8], F32, "gpre")
        self.psT = cx.ps([128, 8, 128], BF16, "psT")
        self.has_pre_norm = g_pre_ap is not None
        if g_pre_ap is not None:
            rec.add("sp", "dma_start", out=self.gpre[:], in_=col_view(g_pre_ap), allow_slow_non_contiguous=True,
                    w=["gpre"], dma=True, sem="gpre")
        if not no_post:
            rec.add("sp", "dma_start", out=self.gpost[:], in_=g_post_ap.partition_broadcast(128),
                    w=["gpost"], dma=True, sem="gpost")
        self.npre = 0
        self.npost = 0

    def rstd(self, ss_ap, out_ap, scale, rkeys, wkeys):
        rec = self.rec
        sc2 = float(scale) ** 2
        rec.add("act", "activation", out=out_ap, in_=ss_ap, func=AF.Sqrt, scale=1.0 / (D * sc2), bias=EPS / sc2,
                r=rkeys, w=wkeys)
        rec.add("dve", "reciprocal", out=out_ap, in_=out_ap, r=wkeys, w=wkeys)

    def pre_tile(self, g, src=None):
        rec = self.rec
        i = self.npre
        self.npre += 1
        s = i % 2
        src = self.src if src is None else src
        rows = src[g * 128:(g + 1) * 128, :]
        xb = self.xsb[s]
        if self.pre_from_bf16:
            rec.add("sp", "dma_start", out=xb[:], in_=rows, w=[("xsb", s)], dma=True, sem=f"xpre{s}")
            return s
        xt = self.xpre[s]
        rec.add("sp", "dma_start", out=xt[:], in_=rows, w=[("xpre", s)], dma=True, sem=f"xpre{s}")
        ss = self.stat[:, s:s + 1]
        rs = self.stat[:, 2 + s:3 + s]
        rec.add("act", "activation", out=self.sq[:], in_=xt[:], func=AF.Square, accum_out=ss,
                r=[("xpre", s)], w=[("sq", 0), ("sq", 1), ("ss", s)])
        self.rstd(ss, rs, 1.0, [("ss", s)], [("rs", s)])
        rec.add("act", "activation", out=xb[:], in_=xt[:], func=AF.Copy, scale=rs,
                r=[("xpre", s), ("rs", s)], w=[("xsb", s)])
        return s

    def transpose_tile(self, s, blk_slot, t):
        rec = self.rec
        xb = self.xsb[s]
        pT = self.psT
        for c in range(8):
            rec.add("pe", "transpose", out=pT[:, c, :], in_=xb[:, c * 128:(c + 1) * 128], identity=self.ident[:],
                    r=[("xsb", s), "ident"], w=["psT"])
        dstT = self.xnT[blk_slot][:, :, t * 128:(t + 1) * 128]
        if self.has_pre_norm:
            gb = self.gpre[:].unsqueeze(2).to_broadcast([128, 8, 128])
            rec.add("dve", "tensor_tensor", out=dstT, in0=pT[:], in1=gb, op=ALU.mult,
                    r=["psT", "gpre"], w=[("xnT", blk_slot, t)])
        else:
            rec.add("dve", "tensor_copy", out=dstT, in_=pT[:], r=["psT"], w=[("xnT", blk_slot, t)])

    def pre_block(self, b, slot, src=None):
        for ph in self.pre_block_phases(b, slot, src):
            ph()

    def pre_block_phases(self, b, slot, src=None):
        st = {}
        g0 = b * self.tpb

        def p0():
            st[0] = self.pre_tile(g0 + 0, src)
            st[1] = self.pre_tile(g0 + 1, src)

        def p1():
            self.transpose_tile(st[0], slot, 0)
            st[2] = self.pre_tile(g0 + 2, src)
            self.transpose_tile(st[1], slot, 1)
            st[3] = self.pre_tile(g0 + 3, src)

        def p2():
            self.transpose_tile(st[2], slot, 2)
            self.transpose_tile(st[3], slot, 3)

        return [p0, p1, p2]

    def xnT_keys(self, slot):
        return [("xnT", slot, t) for t in range(self.tpb)]

    def post_load(self, g):
        rec = self.rec
        i = self.npost
        self.npost += 1
        s = i % 2
        rows = self.src[g * 128:(g + 1) * 128, :]
        rec.add("sp", "dma_start", out=self.xpost[s][:], in_=rows, w=[("xpost", s)], dma=True, sem=f"xpost{s}")
        return s

    def post_tile(self, g, s, ps_out, ps_key, pkeys=None):
        rec = self.rec
        xt = self.xpost[s]
        ss = self.stat[:, 4 + s:5 + s]
        rs = self.stat[:, 6 + s:7 + s]
        pk = [(ps_key, 0), (ps_key, 1)] if pkeys is None else list(pkeys)
        rec.add("act", "activation", out=self.sq[:], in_=ps_out, func=AF.Square, accum_out=ss,
                r=pk, w=[("sq", 0), ("sq", 1), ("pss", s)])
        self.rstd(ss, rs, self.out_scale, [("pss", s)], [("prs", s)])
        rec.add("dve", "tensor_tensor", out=self.tmp[:], in0=ps_out, in1=self.gpost[:], op=ALU.mult,
                r=pk + ["gpost"], w=["tmp"])
        rec.add("dve", "scalar_tensor_tensor", out=xt[:], in0=self.tmp[:], scalar=rs, in1=xt[:],
                op0=ALU.mult, op1=ALU.add, r=["tmp", ("prs", s), ("xpost", s)], w=[("xpost", s)])
        rows = self.dst[g * 128:(g + 1) * 128, :]
        rec.add("sp", "dma_start", out=rows, in_=xt[:], r=[("xpost", s)], dma=True, sem=f"xst{s}")


def load_w_rows(rec, wt, w_ap, key, nk, col_splits=1):
    wv = w_ap.rearrange("(c p) f -> p c f", p=128)
    F = wv.shape[2]
    step = F // col_splits
    for q in range(col_splits):
        for c in range(nk):
            rec.add("pool", "dma_start", out=wt[:, c, q * step:(q + 1) * step], in_=wv[:, c, q * step:(q + 1) * step],
                    w=[(key, c, q)], dma=True, sem=f"{key}_{c}_{q}")


def ffn_stage(nc, name, consts, src, dst, nt, w_in, w_out, g_pre, g_post, dbg=None):
    import contextlib
    with contextlib.ExitStack() as st:
        cx = Ctx(nc, name, st)
        rec = cx.rec
        ts = TokenStage(cx, consts, src, dst, g_pre, g_post, 0.5)
        tb, tpb = ts.tb, ts.tpb
        nblk = nt // tb
        win = cx.sb([128, 8, 2 * DFF], BF16, "win")
        wout = cx.sb([128, NCH, D], BF16, "wout")
        aT = cx.sb([128, NCH, tb], BF16, "aT")
        gs = [cx.sb([128, tb], BF16, "gs") for _ in range(2)]
        psGU = cx.ps([128, 4, tb], F32, "psGU")
        psG = [psGU[:, 0, :], psGU[:, 1, :]]
        psU = [psGU[:, 2, :], psGU[:, 3, :]]
        psO = cx.ps([128, D], F32, "psO")
        alt = [psGU[:, 0:2, :].rearrange("p a b -> p (a b)"), psGU[:, 2:4, :].rearrange("p a b -> p (a b)")]
        bnd = [0, 6, 12, 17, NCH]
        win_v = w_in.rearrange("(c p) f -> p c f", p=128)
        for qi in range(4):
            for half in range(2):
                c0 = half * DFF + bnd[qi] * 128
                c1 = half * DFF + bnd[qi + 1] * 128
                for c in range(8):
                    rec.add("pool", "dma_start", out=win[:, c, c0:c1], in_=win_v[:, c, c0:c1],
                            w=[("win", c, half * 4 + qi)], dma=True, sem=f"win_{half}_{qi}")
        qof = [max(i for i in range(4) if bnd[i] <= j) for j in range(NCH)]
        load_w_rows(rec, wout, w_out, "wout", NCH, col_splits=1)

        ts.pre_block(0, 0)
        for b in range(nblk):
            slot = b % 2
            xnT = ts.xnT[slot]
            for j in range(NCH):
                pg, pu = psG[j % 2], psU[j % 2]
                for half, pt in ((0, pg), (1, pu)):
                    col = half * DFF + j * 128
                    q = half * 4 + qof[j]
                    for c in range(8):
                        rec.add("pe", "matmul", out=pt, lhsT=win[:, c, col:col + 128], rhs=xnT[:, c, :],
                                start=(c == 0), stop=(c == 7),
                                r=[("win", 7, q)] + (ts.xnT_keys(slot) if c == 0 else []),
                                w=[("psGU", half, j % 2)])
                g_ = gs[j % 2]
                rec.add("act", "activation", out=g_[:], in_=pg, func=AF.Silu,
                        r=[("psGU", 0, j % 2)], w=[("gs", j % 2)])
                rec.add("dve", "tensor_tensor", out=aT[:, j, :], in0=pu, in1=g_[:], op=ALU.mult,
                        r=[("psGU", 1, j % 2), ("gs", j % 2)], w=[("aT", j)])
                if b + 1 < nblk:
                    if j == 5:
                        phs = ts.pre_block_phases(b + 1, 1 - slot)
                        phs[0]()
                    elif j == 11:
                        phs[1]()
                    elif j == 16:
                        phs[2]()
            for t in range(tpb):
                g = b * tpb + t
                s = ts.post_load(g)
                if t % 2 == 0:
                    po, keys = psO[:], [("psO", 0), ("psO", 1)]
                else:
                    ai = t // 2 % 2
                    po, keys = alt[ai], [("psGU", ai, 0), ("psGU", ai, 1)]
                for hf in range(2):
                    for j in range(NCH):
                        rec.add("pe", "matmul", out=po[:, hf * 512:(hf + 1) * 512],
                                lhsT=aT[:, j, t * 128:(t + 1) * 128], rhs=wout[:, j, hf * 512:(hf + 1) * 512],
                                start=(j == 0), stop=(j == NCH - 1),
                                r=[("aT", j), ("wout", j, 0)], w=[keys[hf]])
                ts.post_tile(g, s, po, "psO", pkeys=keys)
        if dbg is not None:
            for nm, t in (("xnT", ts.xnT[0]), ("aT", aT), ("stat", ts.stat), ("tmp", ts.tmp), ("xsb", ts.xsb[0]),
                          ("gpre", ts.gpre), ("gpost", ts.gpost)):
                rec.add("sp", "dma_start", out=dbg[nm], in_=t[:], r=list(rec.last_w.keys()), dma=True, sem="dbg" + nm)
        rec.emit(nc, name)


def xattn_stage(nc, name, consts, src, dst, seqs, mem, w_q, w_kv, w_o, g_pre, g_mem, g_post):
    import contextlib
    with contextlib.ExitStack() as st:
        cx = Ctx(nc, name, st)
        rec = cx.rec
        ts = TokenStage(cx, consts, src, dst, g_pre, g_post, 1.0)
        tb, tpb = ts.tb, ts.tpb
        nseq = len(seqs)
        wq = cx.sb([128, 8, D], BF16, "wq")
        wkv = cx.sb([128, 8, 2 * D], BF16, "wkv")
        wo = cx.sb([128, 8, D], BF16, "wo")
        KT = [cx.sb([128, 8, N_MEM], BF16, "KT") for _ in range(nseq)]
        V = [cx.sb([128, 2, D], BF16, "V") for _ in range(nseq)]
        memT = cx.sb([128, 8, N_MEM], BF16, "memT")
        gmem = cx.sb([128, 8], F32, "gmem")
        qT = cx.sb([128, 8, tb], BF16, "qT")
        pT = cx.sb([128, 8, tb], BF16, "pT")
        oT = cx.sb([128, 8, tb], BF16, "oT")
        pe2 = [cx.sb([128, 4, N_MEM], BF16, "pexp") for _ in range(2)]
        pn2 = [cx.sb([128, 4, N_MEM], BF16, "pn") for _ in range(2)]
        sm2 = [cx.sb([128, 16], F32, "sm") for _ in range(2)]
        psO = cx.ps([128, D], F32, "psQO")
        psQ = [psO[:, 0:tb], psO[:, tb:2 * tb]]
        psS2 = [cx.ps([128, 4, N_MEM], F32, "psS") for _ in range(2)]
        psPT = ts.psT
        load_w_rows(rec, wkv, w_kv, "wkv", 8)
        load_w_rows(rec, wq, w_q, "wq", 8)
        load_w_rows(rec, wo, w_o, "wo", 8)
        rec.add("sp", "dma_start", out=gmem[:], in_=col_view(g_mem), allow_slow_non_contiguous=True,
                w=["gmem"], dma=True, sem="gmem")
        wkv_keys = [("wkv", c, 0) for c in range(8)]
        nq = 0
        for si in range(nseq):
            for mt in range(2):
                s = ts.pre_tile(si * 2 + mt, src=mem)
                for c in range(8):
                    rec.add("pe", "transpose", out=ts.psT[:, c, :], in_=ts.xsb[s][:, c * 128:(c + 1) * 128],
                            identity=ts.ident[:], r=[("xsb", s), "ident"], w=["psT"])
                rec.add("dve", "tensor_tensor", out=memT[:, :, mt * 128:(mt + 1) * 128], in0=ts.psT[:],
                        in1=gmem[:].unsqueeze(2).to_broadcast([128, 8, 128]), op=ALU.mult,
                        r=["psT", "gmem"], w=[("memT", mt)])
            for m in range(8):
                pq = psQ[nq % 2]
                for c in range(8):
                    rec.add("pe", "matmul", out=pq[:, 0:N_MEM], lhsT=wkv[:, c, m * 128:(m + 1) * 128], rhs=memT[:, c, :],
                            start=(c == 0), stop=(c == 7),
                            r=[("wkv", c, 0), ("memT", 0), ("memT", 1)], w=[("psQ", nq % 2)])
                rec.add("act" if m % 2 else "dve", "activation" if m % 2 else "tensor_copy",
                        **(dict(func=AF.Copy) if m % 2 else {}), out=KT[si][:, m, :], in_=pq[:, 0:N_MEM],
                        r=[("psQ", nq % 2)], w=[("KT", si)])
                nq += 1
            for mt in range(2):
                for hf in range(2):
                    pq = psQ[nq % 2]
                    for c in range(8):
                        rec.add("pe", "matmul", out=pq, lhsT=memT[:, c, mt * 128:(mt + 1) * 128],
                                rhs=wkv[:, c, D + hf * 512:D + (hf + 1) * 512], start=(c == 0), stop=(c == 7),
                                r=[("wkv", c, 0), ("memT", mt)], w=[("psQ", nq % 2)])
                    rec.add("act" if hf else "dve", "activation" if hf else "tensor_copy",
                            **(dict(func=AF.Copy) if hf else {}), out=V[si][:, mt, hf * 512:(hf + 1) * 512], in_=pq,
                            r=[("psQ", nq % 2)], w=[("V", si)])
                    nq += 1
        blocks = []
        for si, (t0, L) in enumerate(seqs):
            assert t0 % tb == 0 and L % tb == 0
            for b in range(t0 // tb, (t0 + L) // tb):
                blocks.append((b, si))
        ts.pre_block(blocks[0][0], 0)
        for bi, (b, si) in enumerate(blocks):
            slot = bi % 2
            xnT = ts.xnT[slot]
            for m in range(8):
                pq = psQ[nq % 2]
                for c in range(8):
                    rec.add("pe", "matmul", out=pq, lhsT=wq[:, c, m * 128:(m + 1) * 128], rhs=xnT[:, c, :],
                            start=(c == 0), stop=(c == 7),
                            r=[("wq", c, 0)] + (ts.xnT_keys(slot) if c == 0 else []), w=[("psQ", nq % 2)])
                rec.add("act" if m % 2 else "dve", "activation" if m % 2 else "tensor_copy",
                        **(dict(func=AF.Copy) if m % 2 else {}), out=qT[:, m, :], in_=pq,
                        r=[("psQ", nq % 2)], w=[("qT", m)])
                nq += 1

            def s_steps(t):
                tsl = slice(t * 128, (t + 1) * 128)
                u = t % 2
                psS, pe_, pn, sm = psS2[u], pe2[u], pn2[u], sm2[u]
                A = lambda *a, **kw: (a, kw)
                mx, nb, rsum, rinv = sm[:, 0:4], sm[:, 4:8], sm[:, 8:12], sm[:, 12:16]
                st_ = []
                mm = []
                for h in range(4):
                    for c2 in range(2):
                        mm.append(A("pe", "matmul", out=psS[:, h, :], lhsT=qT[:, 2 * h + c2, tsl], rhs=KT[si][:, 2 * h + c2, :],
                                    start=(c2 == 0), stop=(c2 == 1), r=[("qT", 2 * h + c2), ("KT", si)], w=[("psS", u)]))
                st_.append(mm)
                st_.append([A("dve", "tensor_reduce", out=mx, in_=psS[:], axis=AX.X, op=ALU.max, r=[("psS", u)], w=[("mx", u)])])
                st_.append([A("dve", "tensor_scalar", out=nb, in0=mx, scalar1=-1.0 / 16.0, scalar2=None, op0=ALU.mult,
                              r=[("mx", u)], w=[("nb", u)])])
                st_.append([A("act", "activation", out=pe_[:, h, :], in_=psS[:, h, :], func=AF.Exp, scale=1.0 / 16.0,
                              bias=nb[:, h:h + 1], accum_out=rsum[:, h:h + 1], r=[("psS", u), ("nb", u)],
                              w=[("pexp", u), ("rsum", u, h)]) for h in range(4)])
                st_.append([A("dve", "reciprocal", out=rinv, in_=rsum, r=[("rsum", u, h) for h in range(4)], w=[("rinv", u)])])
                st_.append([A("dve", "tensor_tensor", out=pn[:], in0=pe_[:], in1=rinv.unsqueeze(2).to_broadcast([128, 4, N_MEM]),
                              op=ALU.mult, r=[("pexp", u), ("rinv", u)], w=[("pn", u)])])
                return st_

            def emit_pt(t):
                tsl = slice(t * 128, (t + 1) * 128)
                u = t % 2
                pn = pn2[u]
                for h in range(4):
                    for mc in range(2):
                        rec.add("pe", "transpose", out=psPT[:, 2 * h + mc, :], in_=pn[:, h, mc * 128:(mc + 1) * 128],
                                identity=ts.ident[:], r=[("pn", u), "ident"], w=["psT"])
                rec.add("act", "activation", func=AF.Copy, out=pT[:, :, tsl], in_=psPT[:], r=["psT"], w=[("pT", t)])

            phs = ts.pre_block_phases(blocks[bi + 1][0], 1 - slot) if bi + 1 < len(blocks) else None
            if phs:
                phs[0]()
            for t0_ in range(0, tpb, 2):
                chains = [s_steps(t0_), s_steps(t0_ + 1)]
                for k_ in range(len(chains[0])):
                    for c_ in chains:
                        for (a_, kw_) in c_[k_]:
                            rec.add(*a_, **kw_)
                emit_pt(t0_)
                emit_pt(t0_ + 1)
                if phs:
                    phs[1 + t0_ // 2]()
            for h in range(4):
                for dc in range(2):
                    pq = psQ[nq % 2]
                    for mc in range(2):
                        rec.add("pe", "matmul", out=pq, lhsT=V[si][:, mc, h * 256 + dc * 128:h * 256 + (dc + 1) * 128],
                                rhs=pT[:, 2 * h + mc, :], start=(mc == 0), stop=(mc == 1),
                                r=[("V", si)] + [("pT", t) for t in range(tpb)], w=[("psQ", nq % 2)])
                    m = 2 * h + dc
                    rec.add("act" if m % 2 else "dve", "activation" if m % 2 else "tensor_copy",
                            **(dict(func=AF.Copy) if m % 2 else {}), out=oT[:, m, :], in_=pq,
                            r=[("psQ", nq % 2)], w=[("oT", m)])
                    nq += 1
            for t in range(tpb):
                g = b * tpb + t
                s = ts.post_load(g)
                if t in (1, 2):
                    u = t - 1
                    po, keys = psS2[u][:].rearrange("p a b -> p (a b)"), [("psS", u), ("psS", u)]
                else:
                    po, keys = psO[:], [("psQ", 0), ("psQ", 1)]
                for hf in range(2):
                    for c in range(8):
                        rec.add("pe", "matmul", out=po[:, hf * 512:(hf + 1) * 512], lhsT=oT[:, c, t * 128:(t + 1) * 128],
                                rhs=wo[:, c, hf * 512:(hf + 1) * 512], start=(c == 0), stop=(c == 7),
                                r=[("oT", c), ("wo", c, 0)], w=[keys[hf]])
                ts.post_tile(g, s, po, "psQ", pkeys=keys)
        rec.emit(nc, name)


HD = 8
CH = 64


def lb_prologue(nc, consts, lb_logits, depth):
    import contextlib
    with contextlib.ExitStack() as st:
        cx = Ctx(nc, "lbp", st)
        rec = cx.rec
        lg = cx.sb([128, depth, 8], F32, "lg")
        mx = cx.sb([128, 8], F32, "mx")
        ex = cx.sb([128, depth, 8], F32, "ex")
        sm = cx.sb([128, 8], F32, "sm")
        cs = cx.sb([128, 8], F32, "cs")
        lb, oml = consts["lb"], consts["oml"]
        rec.add("sp", "dma_start", out=lg[:], in_=lb_logits.rearrange("l (c p) -> p l c", p=128),
                allow_slow_non_contiguous=True, w=["lg"], dma=True, sem="lg")
        rec.add("dve", "tensor_copy", out=mx[:], in_=lg[:, 0, :], r=["lg"], w=["mx"])
        for l in range(1, depth):
            rec.add("dve", "tensor_tensor", out=mx[:], in0=mx[:], in1=lg[:, l, :], op=ALU.max, r=["lg", "mx"], w=["mx"])
        for l in range(depth):
            rec.add("dve", "tensor_tensor", out=ex[:, l, :], in0=lg[:, l, :], in1=mx[:], op=ALU.subtract,
                    r=["lg", "mx"], w=["ex"])
        rec.add("act", "activation", out=ex[:], in_=ex[:], func=AF.Exp, r=["ex"], w=["ex"])
        rec.add("dve", "tensor_copy", out=sm[:], in_=ex[:, 0, :], r=["ex"], w=["sm"])
        for l in range(1, depth):
            rec.add("dve", "tensor_tensor", out=sm[:], in0=sm[:], in1=ex[:, l, :], op=ALU.add, r=["ex", "sm"], w=["sm"])
        rec.add("dve", "reciprocal", out=sm[:], in_=sm[:], r=["sm"], w=["sm"])
        for l in range(depth):
            rec.add("dve", "tensor_tensor", out=ex[:, l, :], in0=ex[:, l, :], in1=sm[:], op=ALU.mult,
                    r=["ex", "sm"], w=["ex"])
        for l in range(depth):
            if l == 0:
                rec.add("dve", "tensor_copy", out=cs[:], in_=ex[:, 0, :], r=["ex"], w=["cs"])
            else:
                rec.add("dve", "tensor_tensor", out=cs[:], in0=cs[:], in1=ex[:, l, :], op=ALU.add, r=["ex", "cs"], w=["cs"])
            rec.add("dve", "tensor_tensor", out=lb[:, l, :], in0=cs[:], in1=ex[:, 0, :], op=ALU.subtract,
                    r=["cs", "ex"], w=["lb"])
            rec.add("dve", "tensor_scalar", out=lb[:, l, :], in0=lb[:, l, :], scalar1=0.0, scalar2=None, op0=ALU.max,
                    r=["lb"], w=["lb"])
            rec.add("dve", "tensor_scalar", out=oml[:, l, :], in0=lb[:, l, :], scalar1=-1.0, scalar2=1.0,
                    op0=ALU.mult, op1=ALU.add, r=["lb"], w=["oml"])
        rec.emit(nc, "lbp")


def h1_stage(nc, name, consts, src, nt, w_in, g_pre, gnorm, layer, scr):
    import contextlib
    with contextlib.ExitStack() as st:
        cx = Ctx(nc, name, st)
        rec = cx.rec
        ts = TokenStage(cx, consts, src, None, g_pre, None, 1.0, no_post=True)
        tb, tpb = ts.tb, ts.tpb
        nblk = nt // tb
        ncb = tb // CH
        w = cx.sb([128, 8, 5 * D], BF16, "w")
        load_w_rows(rec, w, w_in, "w", 8, col_splits=5)
        gnt = cx.sb([128, D], F32, "gnt")
        rec.add("sp", "dma_start", out=gnt[:].rearrange("p (h d) -> p h d", h=HD),
                in_=gnorm.partition_broadcast(128).unsqueeze(1).to_broadcast([128, HD, 128]),
                w=["gnt"], dma=True, sem="gnt")
        lbc, omlc = consts["lb"], consts["oml"]
        rmask = cx.sb([128, 512], F32, "rmask")
        rec.add("sp", "dma_start", out=rmask[:], in_=consts["d_rmask"], w=["rmask"], dma=True, sem="rmask")
        psZ = [cx.ps([128, tb], F32, "psZ") for _ in range(4)]
        psV = [cx.ps([128, tb], F32, "psV") for _ in range(2)]
        qs = cx.sb([128, tb], F32, "qs")
        T = {}
        for d in range(2):
            for nm in ("f", "lc", "k", "bc", "ep", "em", "B", "e3"):
                T[(nm, d)] = cx.sb([128, tb], F32, nm)
            for nm in ("qd", "ke", "kd"):
                T[(nm, d)] = [cx.sb([128, tb], BF16, nm) for _ in range(2)]
            T[("bt", d)] = cx.sb([128, ncb], F32, "bt")
            T[("eb", d)] = [cx.sb([128, ncb], F32, "eb") for _ in range(2)]
        vst = [cx.sb([128, D], BF16, "vst") for _ in range(2)]
        gst = [cx.sb([128, D], BF16, "gst") for _ in range(2)]
        gsi = cx.sb([128, D], F32, "gsi")
        nz = 0
        nv = 0
        nout = 0
        ts.pre_block(0, 0)
        for b in range(nblk):
            slot = b % 2
            xnT = ts.xnT[slot]
            tok = slice(b * tb, (b + 1) * tb)
            for hd in range(HD):
                pz = []
                for which in range(3):
                    p = psZ[nz % 4]
                    pk = ("psZ", nz % 4)
                    nz += 1
                    col = which * D + hd * 128
                    for c in range(8):
                        rec.add("pe", "matmul", out=p[:], lhsT=w[:, c, col:col + 128], rhs=xnT[:, c, :],
                                start=(c == 0), stop=(c == 7),
                                r=[("w", c, which)] + (ts.xnT_keys(slot) if c == 0 else []), w=[pk])
                    pz.append((p, pk))
                rec.add("act", "activation", out=qs[:], in_=pz[0][0][:], func=AF.Silu, r=[pz[0][1]], w=["qs"])
                lb_h = lbc[:, layer, hd:hd + 1]
                oml_h = omlc[:, layer, hd:hd + 1]
                chains = []
                for d in range(2):
                    p, pk = pz[1 + d]
                    f, lc, k, bc, ep, em, B, e3 = (T[(nm, d)] for nm in ("f", "lc", "k", "bc", "ep", "em", "B", "e3"))
                    o = nout % 2
                    nout += 1
                    qd, ke, kd, eb = T[("qd", d)][o], T[("ke", d)][o], T[("kd", d)][o], T[("eb", d)][o]
                    bt = T[("bt", d)]
                    K = lambda nm, d=d: (nm, d)
                    KO = lambda nm, d=d, o=o: (nm, d, o)
                    bc3 = bc[:].rearrange("p (c t) -> p c t", t=CH)
                    lc3 = lc[:].rearrange("p (c t) -> p c t", t=CH)
                    B3 = B[:].rearrange("p (c t) -> p c t", t=CH)
                    btb = bt[:].unsqueeze(2).to_broadcast([128, ncb, CH])
                    st_ = []
                    A = lambda *a, **kw: (a, kw)
                    st_.append([A("act", "activation", out=f[:], in_=p[:], func=AF.Sigmoid, r=[pk], w=[K("f")])])
                    st_.append([A("dve", "tensor_scalar", out=f[:], in0=f[:], scalar1=oml_h, scalar2=lb_h, op0=ALU.mult,
                                  op1=ALU.add, r=[K("f"), "lb", "oml"], w=[K("f")])])
                    st_.append([A("act", "activation", out=lc[:], in_=f[:], func=AF.Ln, r=[K("f")], w=[K("lc")]),
                                A("pool", "tensor_scalar", out=k[:], in0=f[:], scalar1=-1.0, scalar2=1.0, op0=ALU.mult,
                                  op1=ALU.add, r=[K("f")], w=[K("k")])])
                    st_.append([A("dve", "tensor_tensor_scan", out=bc[:], data0=rmask[:], data1=lc[:], initial=0.0,
                                  op0=ALU.mult, op1=ALU.add, r=[K("lc"), "rmask"], w=[K("bc")])])
                    st_.append([A("dve", "tensor_copy", out=bt[:], in_=bc3[:, :, CH - 1], r=[K("bc")], w=[K("bt")])])
                    st_.append([A("dve", "tensor_tensor", out=B3, in0=btb, in1=bc3, op=ALU.subtract, r=[K("bt"), K("bc")], w=[K("B")]),
                                A("act", "activation", out=eb[:], in_=bt[:], func=AF.Exp, r=[K("bt")], w=[KO("eb")])])
                    if d == 1:
                        st_.append([A("dve", "tensor_tensor", out=bc3, in0=B3, in1=lc3, op=ALU.add, r=[K("B"), K("lc")], w=[K("bc")])])
                        st_.append([A("dve", "tensor_tensor", out=B3, in0=btb, in1=bc3, op=ALU.subtract, r=[K("bt"), K("bc")], w=[K("B")])])
                    else:
                        st_.append([])
                        st_.append([])
                    st_.append([A("act", "activation", out=ep[:], in_=bc[:], func=AF.Exp, r=[K("bc")], w=[K("ep")])])
                    st_.append([A("act", "activation", out=em[:], in_=bc[:], func=AF.Exp, scale=-1.0, r=[K("bc")], w=[K("em")]),
                                A("pool", "tensor_tensor", out=qd[:], in0=qs[:], in1=ep[:], op=ALU.mult, r=["qs", K("ep")], w=[KO("qd")])])
                    st_.append([A("act", "activation", out=e3[:], in_=B[:], func=AF.Exp, r=[K("B")], w=[K("e3")]),
                                A("pool", "tensor_tensor", out=ke[:], in0=k[:], in1=em[:], op=ALU.mult, r=[K("k"), K("em")], w=[KO("ke")])])
                    st_.append([A("pool", "tensor_tensor", out=kd[:], in0=k[:], in1=e3[:], op=ALU.mult, r=[K("k"), K("e3")], w=[KO("kd")])])
                    fin = []
                    for nm, tl in (("qd", qd), ("ke", ke), ("kd", kd)):
                        fin.append(A("sp", "dma_start", out=scr["h" + nm][d, hd, :, tok], in_=tl[:], r=[KO(nm)], dma=True,
                                     sem=f"o{nm}{d}{o}"))
                    fin.append(A("sp", "dma_start", out=scr["heb"][d, hd, :, b * ncb:(b + 1) * ncb], in_=eb[:], r=[KO("eb")],
                                 dma=True, sem=f"oeb{d}{o}"))
                    st_.append(fin)
                    chains.append(st_)
                for k_ in range(max(len(c_) for c_ in chains)):
                    for c_ in chains:
                        if k_ < len(c_):
                            for (a_, kw_) in c_[k_]:
                                rec.add(*a_, **kw_)
            if b + 1 < nblk:
                ts.pre_block(b + 1, 1 - slot)
            for t in range(tpb):
                g = b * tpb + t
                rows = slice(g * 128, (g + 1) * 128)
                o = g % 2
                for which, stg, dstn in ((3, vst[o], "hv"), (4, gst[o], "hg")):
                    for hf in range(2):
                        p = psV[nv % 2]
                        pk = ("psV", nv % 2)
                        nv += 1
                        col = which * D + hf * 512
                        for c in range(8):
                            rec.add("pe", "matmul", out=p[:], lhsT=xnT[:, c, t * 128:(t + 1) * 128], rhs=w[:, c, col:col + 512],
                                    start=(c == 0), stop=(c == 7), r=[("w", c, which), ("xnT", slot, t)], w=[pk])
                        if which == 3:
                            rec.add("act", "activation", out=stg[:, hf * 512:(hf + 1) * 512], in_=p[:], func=AF.Copy,
                                    r=[pk], w=[("vst", o, hf)])
                        else:
                            rec.add("act", "activation", out=gsi[:, hf * 512:(hf + 1) * 512], in_=p[:], func=AF.Silu,
                                    r=[pk], w=[("gsi", hf)])
                            rec.add("pool", "tensor_tensor", out=stg[:, hf * 512:(hf + 1) * 512],
                                    in0=gsi[:, hf * 512:(hf + 1) * 512], in1=gnt[:, hf * 512:(hf + 1) * 512], op=ALU.mult,
                                    r=[("gsi", hf), "gnt"], w=[("gst", o, hf)])
                    kk = "vst" if which == 3 else "gst"
                    rec.add("sp", "dma_start", out=scr[dstn][rows, :], in_=stg[:], r=[(kk, o, 0), (kk, o, 1)], dma=True,
                            sem=f"o{kk}{o}")
        rec.emit(nc, name)


def h2_stage(nc, name, consts, seqs, scr):
    import contextlib
    with contextlib.ExitStack() as st:
        cx = Ctx(nc, name, st)
        rec = cx.rec
        tb = 512
        ncb = tb // CH
        ident = consts["ident"]
        trif, trib = consts["trif"], consts["trib"]
        QD = [cx.sb([128, HD, tb], BF16, "QD") for _ in range(2)]
        KE = [cx.sb([128, HD, tb], BF16, "KE") for _ in range(2)]
        KD = [cx.sb([128, HD, tb], BF16, "KD") for _ in range(2)]
        EB = [cx.sb([128, HD, ncb], F32, "EB") for _ in range(2)]
        VV = [cx.sb([CH, ncb, D], BF16, "VV") for _ in range(2)]
        OB = cx.sb([CH, ncb, D], F32, "OB")
        GG = cx.sb([CH, ncb, D], BF16, "GG")
        OG = cx.sb([CH, ncb, D], BF16, "OG")
        S = cx.sb([128, HD, 128], F32, "S")
        Sb = cx.sb([128, HD, 128], BF16, "Sb")
        sT = [cx.sb([CH, HD, CH], BF16, "sT") for _ in range(2)]
        kdt = [cx.sb([CH, HD, 128], BF16, "kdt") for _ in range(2)]
        osum = cx.sb([CH, D], F32, "osum")
        osq = cx.sb([CH, D], F32, "osq")
        st8 = cx.sb([CH, 16], F32, "st8")
        pss = cx.ps([CH, HD, CH], F32, "ps_s")
        psk = cx.ps([CH, HD, 128], BF16, "ps_k")
        pso = cx.ps([CH, D], F32, "ps_o")
        psu = cx.ps([128, HD, 128], F32, "ps_u")
        HH = HD // 2
        steps = []
        nld = 0
        for si, (t0, L) in enumerate(seqs):
            nb = L // tb
            for d in (1, 0):
                blks = list(range(nb)) if d == 0 else list(range(nb - 1, -1, -1))
                first = True
                for b in blks:
                    sl = nld % 2
                    nld += 1
                    chunks = list(range(ncb)) if d == 0 else list(range(ncb - 1, -1, -1))
                    for ci, c in enumerate(chunks):
                        steps.append(dict(d=d, T0=t0 + b * tb, sl=sl, c=c, first=first, load=(ci == 0), store=(ci == ncb - 1)))
                        first = False

        def loads(stp):
            d, T0, sl = stp["d"], stp["T0"], stp["sl"]
            tok = slice(T0, T0 + tb)
            for nm, tl in (("hqd", QD[sl]), ("hke", KE[sl]), ("hkd", KD[sl])):
                rec.add("sp", "dma_start", out=tl[:], in_=scr[nm][d, :, :, tok].rearrange("h p t -> p h t"),
                        w=[(nm, sl)], dma=True, sem=f"{nm}{sl}")
            rec.add("sp", "dma_start", out=EB[sl][:], in_=scr["heb"][d, :, :, T0 // CH:T0 // CH + ncb].rearrange("h p c -> p h c"),
                    w=[("heb", sl)], dma=True, sem=f"heb{sl}")
            rec.add("sp", "dma_start", out=VV[sl][:], in_=scr["hv"][tok, :].rearrange("(c p) f -> p c f", p=CH),
                    w=[("hv", sl)], dma=True, sem=f"hv{sl}")

        def loads_fwd(stp):
            d, T0 = stp["d"], stp["T0"]
            tok = slice(T0, T0 + tb)
            if d == 0:
                rec.add("sp", "dma_start", out=OB[:], in_=scr["hob"][tok, :].rearrange("(c p) f -> p c f", p=CH),
                        r=[("hobD", T0)], w=[("OB", c) for c in range(ncb)], dma=True, sem="OB")
                rec.add("sp", "dma_start", out=GG[:], in_=scr["hg"][tok, :].rearrange("(c p) f -> p c f", p=CH),
                        w=["GG"], dma=True, sem="GG")

        def stage_a(i):
            stp = steps[i]
            if stp["load"]:
                loads(stp)
            sl, c, d = stp["sl"], stp["c"], stp["d"]
            cs = slice(c * CH, (c + 1) * CH)
            i2 = i % 2
            tri = trif if d == 0 else trib
            for hd in range(HD):
                rec.add("pe", "matmul", out=pss[:, hd, :], lhsT=KE[sl][:, hd, cs], rhs=QD[sl][:, hd, cs], start=True, stop=True,
                        r=[("hke", sl), ("hqd", sl)], w=["ps_s"])
            for hd in range(HD):
                rec.add("pe", "transpose", out=psk[:, hd, :], in_=KD[sl][:, hd, cs], identity=ident[:],
                        r=[("hkd", sl), "ident"], w=["ps_k"])
            rec.add("dve", "tensor_tensor", out=sT[i2][:], in0=pss[:], in1=tri[:].unsqueeze(1).to_broadcast([CH, HD, CH]),
                    op=ALU.mult, r=["ps_s", "tri"], w=[("sT", i2)])
            rec.add("act", "activation", out=kdt[i2][:], in_=psk[:], func=AF.Copy, r=["ps_k"], w=[("kdt", i2)])

        def stage_b(i):
            stp = steps[i]
            sl, c, first = stp["sl"], stp["c"], stp["first"]
            i2 = i % 2
            for hd in range(HD):
                hs = slice(hd * 128, (hd + 1) * 128)
                rec.add("pe", "matmul", out=pso[:, hs], lhsT=sT[i2][:, hd, :], rhs=VV[sl][:, c, hs], start=(hd % HH == 0),
                        stop=first, skip_group_check=True, r=[("sT", i2), ("hv", sl)], w=[("ps_o", hd // HH)])
            for hd in range(HD):
                hs = slice(hd * 128, (hd + 1) * 128)
                rec.add("pe", "matmul", out=psu[:, hd, :], lhsT=kdt[i2][:, hd, :], rhs=VV[sl][:, c, hs], start=True, stop=True,
                        r=[("kdt", i2), ("hv", sl)], w=[("ps_u", hd // HH)])

        def stage_c(i):
            stp = steps[i]
            sl, c, d, first, T0 = stp["sl"], stp["c"], stp["d"], stp["first"], stp["T0"]
            cs = slice(c * CH, (c + 1) * CH)
            tok = slice(T0, T0 + tb)
            if not first:
                for hd in range(HD):
                    hs = slice(hd * 128, (hd + 1) * 128)
                    rec.add("pe", "matmul", out=pso[:, hs], lhsT=QD[sl][:, hd, cs], rhs=Sb[:, hd, :], start=False, stop=True,
                            skip_group_check=True, r=[("hqd", sl), ("Sb", hd // HH)], w=[("ps_o", hd // HH)])
            for hh in range(2):
                hsl = slice(hh * HH, (hh + 1) * HH)
                if first:
                    rec.add("dve", "tensor_copy", out=S[:, hsl, :], in_=psu[:, hsl, :], r=[("ps_u", hh)], w=[("S", hh)])
                else:
                    rec.add("pool", "tensor_tensor", out=S[:, hsl, :], in0=S[:, hsl, :],
                            in1=EB[sl][:, hsl, c:c + 1].to_broadcast([128, HH, 128]), op=ALU.mult,
                            r=[("S", hh), ("heb", sl)], w=[("S", hh)])
                    rec.add("dve", "tensor_tensor", out=S[:, hsl, :], in0=S[:, hsl, :], in1=psu[:, hsl, :], op=ALU.add,
                            r=[("S", hh), ("ps_u", hh)], w=[("S", hh)])
                rec.add("act", "activation", out=Sb[:, hsl, :], in_=S[:, hsl, :], func=AF.Copy, r=[("S", hh)], w=[("Sb", hh)])
            if d == 1:
                for hh in range(2):
                    fs = slice(hh * 512, (hh + 1) * 512)
                    rec.add("act", "activation", out=OB[:, c, fs], in_=pso[:, fs], func=AF.Copy, r=[("ps_o", hh)], w=[("OB", c)])
                if stp["store"]:
                    rec.add("sp", "dma_start", out=scr["hob"][tok, :].rearrange("(c p) f -> p c f", p=CH), in_=OB[:],
                            r=[("OB", c) for c in range(ncb)], w=[("hobD", T0)], dma=True, sem="oOUT")
            else:
                for hh in range(2):
                    fs = slice(hh * 512, (hh + 1) * 512)
                    rec.add("dve", "tensor_tensor", out=osum[:, fs], in0=pso[:, fs], in1=OB[:, c, fs], op=ALU.add,
                            r=[("ps_o", hh), ("OB", c)], w=["osum"])
                rec.add("act", "activation", out=osq[:], in_=osum[:], func=AF.Square, r=["osum"], w=["osq"])
                ss = st8[:, 0:8]
                rs = st8[:, 8:16]
                rec.add("dve", "tensor_reduce", out=ss, in_=osq[:].rearrange("p (h d) -> p h d", h=HD), axis=AX.X,
                        op=ALU.add, r=["osq"], w=["ss"])
                rec.add("act", "activation", out=rs, in_=ss, func=AF.Sqrt, scale=1.0 / 128.0, bias=EPS, r=["ss"], w=["rs"])
                rec.add("dve", "reciprocal", out=rs, in_=rs, r=["rs"], w=["rs"])
                rec.add("pool", "tensor_tensor", out=osq[:].rearrange("p (h d) -> p h d", h=HD),
                        in0=osum[:].rearrange("p (h d) -> p h d", h=HD),
                        in1=rs.unsqueeze(2).to_broadcast([CH, HD, 128]), op=ALU.mult, r=["osum", "rs"], w=["osq"])
                rec.add("pool", "tensor_tensor", out=OG[:, c, :], in0=osq[:], in1=GG[:, c, :], op=ALU.mult,
                        r=["osq", "GG"], w=[("OG", c)])
                if stp["store"]:
                    rec.add("sp", "dma_start", out=scr["hog"][tok, :].rearrange("(c p) f -> p c f", p=CH), in_=OG[:],
                            r=[("OG", c) for c in range(ncb)], dma=True, sem="oOG")

        n = len(steps)
        loads_fwd(steps[0])
        stage_a(0)
        stage_b(0)
        for i in range(n):
            if i + 1 < n:
                stage_a(i + 1)
            stage_c(i)
            if i + 1 < n:
                if steps[i + 1]["load"]:
                    loads_fwd(steps[i + 1])
                stage_b(i + 1)
        rec.emit(nc, name)


def proj_out_stage(nc, name, consts, src, dst, nt, og, w_out, g_post):
    import contextlib
    with contextlib.ExitStack() as st:
        cx = Ctx(nc, name, st)
        rec = cx.rec
        ts = TokenStage(cx, consts, src, dst, None, g_post, 1.0, pre_from_bf16=True)
        tb, tpb = ts.tb, ts.tpb
        nblk = nt // tb
        wo = cx.sb([128, 8, D], BF16, "wo")
        load_w_rows(rec, wo, w_out, "wo", 8)
        psO = [cx.ps([128, D], F32, "psO") for _ in range(2)]
        ts.pre_block(0, 0, src=og)
        for b in range(nblk):
            slot = b % 2
            xnT = ts.xnT[slot]
            if b + 1 < nblk:
                ts.pre_block(b + 1, 1 - slot, src=og)
            for t in range(tpb):
                g = b * tpb + t
                s = ts.post_load(g)
                po = psO[g % 2]
                for hf in range(2):
                    for c in range(8):
                        rec.add("pe", "matmul", out=po[:, hf * 512:(hf + 1) * 512], lhsT=xnT[:, c, t * 128:(t + 1) * 128],
                                rhs=wo[:, c, hf * 512:(hf + 1) * 512], start=(c == 0), stop=(c == 7),
                                r=[("xnT", slot, t), ("wo", c, 0)], w=[("psO", g % 2, hf)])
                ts.post_tile(g, s, po[:], "psO", pkeys=[("psO", g % 2, 0), ("psO", g % 2, 1)])
        rec.emit(nc, name)


DILS = (1, 4, 16)
NH = 16
HDIM = 64
MASK_REL = 60000.0


def attn_const_arrays():
    k = np.arange(128)[:, None]
    q = np.arange(128)[None, :]
    relL = np.abs(q - k + 64).astype(np.float32)
    relR = np.abs(q - k - 64).astype(np.float32)
    L = np.where(relL <= 64, relL, MASK_REL)
    R = np.where(relR <= 64, relR, MASK_REL)
    L0 = np.where(k >= 64, L, MASK_REL)
    Rend = np.where(k < 64, R, MASK_REL)
    rel = np.stack([np.stack([L, R]), np.stack([L0, R]), np.stack([L, Rend]), np.stack([L0, Rend])]).astype(ml_dtypes.bfloat16)
    slopes = np.exp2(-8.0 * np.arange(1, NH + 1, dtype=np.float32) / NH)
    ci = np.zeros((len(DILS), NH, 128, 128), np.float32)
    for gi, d in enumerate(DILS):
        for h in range(NH):
            ci[gi, h] = -8.0 * slopes[h] * d * np.eye(128, dtype=np.float32)
    return rel, ci.astype(ml_dtypes.bfloat16)


def a1_stage(nc, name, consts, src, nt, w_in, g_pre, aqkv):
    import contextlib
    with contextlib.ExitStack() as st:
        cx = Ctx(nc, name, st)
        rec = cx.rec
        ts = TokenStage(cx, consts, src, None, g_pre, None, 1.0, no_post=True)
        tb, tpb = ts.tb, ts.tpb
        nblk = nt // tb
        w = cx.sb([128, 8, 9 * D], BF16, "w")
        load_w_rows(rec, w, w_in, "w", 8, col_splits=3)
        stg = [cx.sb([128, 3 * D], BF16, "stg") for _ in range(2)]
        psA = [cx.ps([128, 512], F32, "psA") for _ in range(4)]
        na = 0
        ns = 0
        ts.pre_block(0, 0)
        for b in range(nblk):
            slot = b % 2
            xnT = ts.xnT[slot]
            for t in range(tpb):
                g = b * tpb + t
                for gi in range(3):
                    so = ns % 2
                    ns += 1
                    for n in range(6):
                        col = gi * 3 * D + n * 512
                        p = psA[na % 4]
                        pk = ("psA", na % 4)
                        na += 1
                        for c in range(8):
                            rec.add("pe", "matmul", out=p[:], lhsT=xnT[:, c, t * 128:(t + 1) * 128], rhs=w[:, c, col:col + 512],
                                    start=(c == 0), stop=(c == 7), r=[("w", c, gi), ("xnT", slot, t)], w=[pk])
                        if n % 2:
                            rec.add("act", "activation", out=stg[so][:, n * 512:(n + 1) * 512], in_=p[:], func=AF.Copy,
                                    r=[pk], w=[("stg", so, n)])
                        else:
                            rec.add("dve", "tensor_copy", out=stg[so][:, n * 512:(n + 1) * 512], in_=p[:],
                                    r=[pk], w=[("stg", so, n)])
                    rec.add("sp", "dma_start", out=aqkv[g * 128:(g + 1) * 128, gi * 3 * D:(gi + 1) * 3 * D], in_=stg[so][:],
                            r=[("stg", so, n) for n in range(6)], dma=True, sem=f"ostg{so}")
                if b + 1 < nblk:
                    if t == 0:
                        phs = ts.pre_block_phases(b + 1, 1 - slot)
                        phs[0]()
                    elif t == 1:
                        phs[1]()
                    elif t == 2:
                        phs[2]()
        rec.emit(nc, name)


def attn_decay_table():
    k = np.arange(128)[:, None]
    q = np.arange(128)[None, :]
    rel = [np.abs(q - k + 64).astype(np.float64), np.abs(q - k - 64).astype(np.float64)]
    slopes = np.exp2(-8.0 * np.arange(1, NH + 1, dtype=np.float64) / NH)
    dm = np.zeros((128, len(DILS), NH // 2, 4, 128), np.float64)
    for gi, d in enumerate(DILS):
        for h in range(NH):
            for side in range(2):
                v = np.where(rel[side] <= 64, np.exp(-slopes[h] * d * rel[side]), 0.0)
                dm[:, gi, h // 2, (h % 2) * 2 + side, :] = v
    return dm.astype(np.float32).astype(ml_dtypes.bfloat16)


HPB = 6


def a2_stage(nc, name, consts, seqs, aqkv, ao, al):
    import contextlib
    with contextlib.ExitStack() as st:
        cx = Ctx(nc, name, st)
        rec = cx.rec
        ident = consts["ident"]
        dm = cx.sb([128, len(DILS), NH // 2, 4, 128], BF16, "dm")
        rec.add("sp", "dma_start", out=dm[:], in_=consts["d_dm"], w=["dm"], dma=True, sem="dm")
        NKS = 6
        NKR = 4
        NQ = 3
        KR = [cx.sb([128, D], BF16, "KR") for _ in range(NKR)]
        QR = [cx.sb([128, D], BF16, "QR") for _ in range(NQ)]
        VA = [cx.sb([128, NH, HDIM + 1], BF16, "VA") for _ in range(NKS)]
        kT = [cx.sb([128, 8, 128], BF16, "kT") for _ in range(NKS)]
        qT = [cx.sb([128, 8, 128], BF16, "qT") for _ in range(NQ)]
        ET = [cx.sb([128, 2, 4, 128], BF16, "ET") for _ in range(2)]
        PT = [cx.sb([128, 2, 4, 128], BF16, "PT") for _ in range(3)]
        OS = [cx.sb([128, D], F32, "OS") for _ in range(2)]
        DS = [cx.sb([128, NH], F32, "DS") for _ in range(2)]
        psTK = cx.ps([128, 8, 128], BF16, "psTK")
        psTQ = psTK
        psS = [cx.ps([128, 2, 512], F32, "psS") for _ in range(2)]
        psO = [cx.ps([128, 512], F32, "psO") for _ in range(3)]
        cnt = dict(kr=0, qr=0, ks=0)
        kstate = {}

        def load_key(gi, t0, n, d, r, j):
            key = (gi, t0, r, j)
            if key in kstate:
                return key
            sl = cnt["ks"] % NKS
            cnt["ks"] += 1
            for k_ in [k_ for k_, v_ in kstate.items() if v_["sl"] == sl]:
                del kstate[k_]
            kr = cnt["kr"] % NKR
            cnt["kr"] += 1
            kstate[key] = dict(sl=sl, kr=kr, ready=False)
            lo = max(0, 128 * j - 64)
            hi = min(n, 128 * j + 64)
            p0 = lo - (128 * j - 64)
            p1 = hi - (128 * j - 64)
            tok0 = t0 + lo * d + r
            rows = slice(tok0, tok0 + (hi - lo - 1) * d + 1, d)
            colk = gi * 3 * D + D
            colv = gi * 3 * D + 2 * D
            if p0 > 0:
                rec.add("pool", "memset", ap=KR[kr][0:p0, :], constant=0.0, w=[("KR", kr)])
                rec.add("pool", "memset", ap=VA[sl][0:p0, :, :], constant=0.0, w=[("VA", sl)])
            if p1 < 128:
                rec.add("pool", "memset", ap=KR[kr][p1:128, :], constant=0.0, w=[("KR", kr)])
                rec.add("pool", "memset", ap=VA[sl][p1:128, :, :], constant=0.0, w=[("VA", sl)])
            rec.add("pool", "memset", ap=VA[sl][p0:p1, :, HDIM:HDIM + 1], constant=1.0, w=[("VA", sl)])
            rec.add("sp", "dma_start", out=KR[kr][p0:p1, :], in_=aqkv[rows, colk:colk + D], r=[("KR", kr)], w=[("KR", kr)],
                    dma=True, sem=f"KR{kr}")
            rec.add("sp", "dma_start", out=VA[sl][p0:p1, :, 0:HDIM],
                    in_=aqkv[rows, colv:colv + D].rearrange("r (h d) -> r h d", h=NH), r=[("VA", sl)], w=[("VA", sl)],
                    dma=True, sem=f"VA{sl}")
            return key

        def trans_key(key):
            st_ = kstate[key]
            sl, kr = st_["sl"], st_["kr"]
            if not st_["ready"]:
                for c in range(8):
                    rec.add("pe", "transpose", out=psTK[:, c, :], in_=KR[kr][:, c * 128:(c + 1) * 128], identity=ident[:],
                            r=[("KR", kr), "ident"], w=["psTK"])
                rec.add("dve", "tensor_copy", out=kT[sl][:], in_=psTK[:], r=["psTK"], w=[("kT", sl)])
                st_["ready"] = True
            return sl

        def prep_loads(blk):
            gi, t0, n, d, r, qb, nb = blk
            kl = load_key(gi, t0, n, d, r, qb)
            kr_ = load_key(gi, t0, n, d, r, qb + 1)
            s = cnt["qr"] % NQ
            cnt["qr"] += 1
            tok0 = t0 + 128 * qb * d + r
            rows = slice(tok0, tok0 + 127 * d + 1, d)
            colq = gi * 3 * D
            rec.add("sp", "dma_start", out=QR[s][:], in_=aqkv[rows, colq:colq + D], w=[("QR", s)], dma=True, sem=f"QR{s}")
            return dict(keys=(kl, kr_), qs=s, rows=rows, gi=gi)

        def prep_trans(info):
            info["sl"] = (trans_key(info["keys"][0]), trans_key(info["keys"][1]))
            s = info["qs"]
            for c in range(8):
                rec.add("pe", "transpose", out=psTQ[:, c, :], in_=QR[s][:, c * 128:(c + 1) * 128], identity=ident[:],
                        r=[("QR", s), "ident"], w=["psTK"])
            rec.add("act", "activation", out=qT[s][:], in_=psTQ[:], func=AF.Copy, r=["psTK"], w=[("qT", s)])

        blocks = []
        for gi, d in enumerate(DILS):
            for (t0, L) in seqs:
                n = L // d
                assert n % 128 == 0
                nb = n // 128
                for r in range(d):
                    for qb in range(nb):
                        blocks.append((gi, t0, n, d, r, qb, nb))

        NP = NH // 4

        def emit_s(info, bi, up, u):
            pb, pt, eb = u % 2, u % 3, u % 2
            for pr in (0, 1):
                hp = 2 * up + pr
                for side in (0, 1):
                    for hh in (0, 1):
                        base = hh * 64
                        ksl = info["sl"][side]
                        c0 = pr * 256 + side * 128
                        rec.add("pe", "matmul", out=psS[pb][:, hh, c0:c0 + 128], lhsT=kT[ksl][base:base + 64, hp, :],
                                rhs=qT[info["qs"]][base:base + 64, hp, :], start=True, stop=True,
                                r=[("kT", ksl), ("qT", info["qs"])], w=[("psS", pb)])
            rec.add("act", "activation", out=ET[eb][:].rearrange("p a b q -> p a (b q)"), in_=psS[pb][:],
                    func=AF.Exp, scale=0.125, r=[("psS", pb)], w=[("ET", eb)])
            for hh in (0, 1):
                dmv = dm[:, info["gi"], 2 * up:2 * up + 2, hh * 2:hh * 2 + 2, :]
                rec.add("dve", "tensor_tensor", out=PT[pt][:, hh].rearrange("p (r s) q -> p r s q", r=2),
                        in0=ET[eb][:, hh].rearrange("p (r s) q -> p r s q", r=2), in1=dmv, op=ALU.mult,
                        r=[("ET", eb), "dm"], w=[("PT", pt, hh)])

        def emit_pv(info, bi, up, u):
            pt = u % 3
            for pr in (0, 1):
                hp = 2 * up + pr
                for hh in (0, 1):
                    h = 2 * hp + hh
                    bk, hi_ = h // HPB, h % HPB
                    for side in (0, 1):
                        ksl = info["sl"][side]
                        rec.add("pe", "matmul", out=psO[bk][:, hi_ * (HDIM + 1):(hi_ + 1) * (HDIM + 1)],
                                lhsT=PT[pt][:, hh, pr * 2 + side, :], rhs=VA[ksl][:, h, :], start=(side == 0), stop=(side == 1),
                                r=[("PT", pt, hh), ("VA", ksl)], w=[("psO", bk)])
                self_evac(info, bi, hp)

        def self_evac(info, bi, hp):
            h_last = 2 * hp + 1
            if h_last % HPB == HPB - 1 or h_last == NH - 1:
                bk = h_last // HPB
                h0 = bk * HPB
                nh = h_last - h0 + 1
                o = bi % 2
                src = psO[bk][:, 0:nh * (HDIM + 1)].rearrange("p (h e) -> p h e", e=HDIM + 1)
                rec.add("dve", "tensor_copy", out=OS[o][:, h0 * HDIM:(h0 + nh) * HDIM].rearrange("p (h d) -> p h d", d=HDIM),
                        in_=src[:, :, 0:HDIM], r=[("psO", bk)], w=[("OS", o, bk)])
                rec.add("dve", "tensor_copy", out=DS[o][:, h0:h0 + nh], in_=src[:, :, HDIM], r=[("psO", bk)], w=[("DS", o, bk)])
                if h_last == NH - 1:
                    gi = info["gi"]
                    rows = info["rows"]
                    nbk = (NH + HPB - 1) // HPB
                    rec.add("sp", "dma_start", out=ao[gi][rows, :], in_=OS[o][:], r=[("OS", o, b_) for b_ in range(nbk)], dma=True,
                            sem=f"oOS{o}")
                    rec.add("sp", "dma_start", out=al[gi][rows, :], in_=DS[o][:], r=[("DS", o, b_) for b_ in range(nbk)], dma=True,
                            sem=f"oDS{o}")

        infos = {0: prep_loads(blocks[0])}
        prep_trans(infos[0])
        if len(blocks) > 1:
            infos[1] = prep_loads(blocks[1])
        pending = None
        u = 0
        for bi, blk in enumerate(blocks):
            info = infos[bi]
            for hp in range(NP):
                emit_s(info, bi, hp, u)
                if pending is not None:
                    emit_pv(*pending)
                pending = (info, bi, hp, u)
                u += 1
                if hp == 0 and bi + 2 < len(blocks):
                    infos[bi + 2] = prep_loads(blocks[bi + 2])
                if hp == 1 and bi + 1 < len(blocks):
                    prep_trans(infos[bi + 1])
            infos.pop(bi - 1, None)
        emit_pv(*pending)
        rec.emit(nc, name)


def a3_stage(nc, name, consts, src, dst, nt, ao, al, w_out, g_post):
    import contextlib
    with contextlib.ExitStack() as st:
        cx = Ctx(nc, name, st)
        rec = cx.rec
        ts = TokenStage(cx, consts, src, dst, None, g_post, 1.0, pre_from_bf16=True)
        tb, tpb = ts.tb, ts.tpb
        nblk = nt // tb
        wo = cx.sb([128, 8, D], BF16, "wo")
        load_w_rows(rec, wo, w_out, "wo", 8)
        psO2 = [cx.ps([128, D], F32, "psO") for _ in range(2)]
        O = [[cx.sb([128, D], F32, "O") for _ in range(3)] for _ in range(2)]
        Lx = [[cx.sb([128, NH], F32, "L") for _ in range(3)] for _ in range(2)]
        npre = [0]

        def pre_block(b, slot):
            for t in range(tpb):
                g = b * tpb + t
                i = npre[0]
                npre[0] += 1
                s = i % 2
                rows = slice(g * 128, (g + 1) * 128)
                for gi in range(3):
                    rec.add("sp", "dma_start", out=O[s][gi][:], in_=ao[gi][rows, :], w=[("O", s, gi)], dma=True, sem=f"O{s}{gi}")
                    rec.add("sp", "dma_start", out=Lx[s][gi][:], in_=al[gi][rows, :], w=[("L", s, gi)], dma=True, sem=f"L{s}{gi}")
                for gi in (1, 2):
                    rec.add("dve", "tensor_tensor", out=O[s][0][:], in0=O[s][0][:], in1=O[s][gi][:], op=ALU.add,
                            r=[("O", s, 0), ("O", s, gi)], w=[("O", s, 0)])
                    rec.add("dve", "tensor_tensor", out=Lx[s][0][:], in0=Lx[s][0][:], in1=Lx[s][gi][:], op=ALU.add,
                            r=[("L", s, 0), ("L", s, gi)], w=[("L", s, 0)])
                rec.add("dve", "reciprocal", out=Lx[s][0][:], in_=Lx[s][0][:], r=[("L", s, 0)], w=[("L", s, 0)])
                rec.add("dve", "tensor_tensor", out=ts.xsb[s][:].rearrange("p (h d) -> p h d", h=NH),
                        in0=O[s][0][:].rearrange("p (h d) -> p h d", h=NH),
                        in1=Lx[s][0][:].unsqueeze(2).to_broadcast([128, NH, HDIM]), op=ALU.mult,
                        r=[("O", s, 0), ("L", s, 0)], w=[("xsb", s)])
                ts.transpose_tile(s, slot, t)

        pre_block(0, 0)
        for b in range(nblk):
            slot = b % 2
            xnT = ts.xnT[slot]
            if b + 1 < nblk:
                pre_block(b + 1, 1 - slot)
            for t in range(tpb):
                g = b * tpb + t
                s = ts.post_load(g)
                psO = psO2[g % 2]
                for hf in range(2):
                    for c in range(8):
                        rec.add("pe", "matmul", out=psO[:, hf * 512:(hf + 1) * 512], lhsT=xnT[:, c, t * 128:(t + 1) * 128],
                                rhs=wo[:, c, hf * 512:(hf + 1) * 512], start=(c == 0), stop=(c == 7),
                                r=[("xnT", slot, t), ("wo", c, 0)], w=[("psO", g % 2, hf)])
                ts.post_tile(g, s, psO[:], "psO", pkeys=[("psO", g % 2, 0), ("psO", g % 2, 1)])
        rec.emit(nc, name)


def const_arrays():
    rm = np.ones((128, 512), np.float32)
    rm[:, ::CH] = 0.0
    s = np.arange(CH)
    return {
        "c_ident": np.eye(128).astype(ml_dtypes.bfloat16),
        "c_rmask": rm,
        "c_trif": (s[:, None] <= s[None, :]).astype(ml_dtypes.bfloat16),
        "c_trib": (s[:, None] >= s[None, :]).astype(ml_dtypes.bfloat16),
        "c_dm": attn_decay_table(),
        "c_ones": np.ones((128, 1), ml_dtypes.bfloat16),
    }


CONST_SPECS = (("ident", [128, 128], BF16), ("rmask", [128, 512], F32), ("trif", [CH, CH], BF16), ("trib", [CH, CH], BF16),
               ("dm", [128, 3, NH // 2, 4, 128], BF16), ("ones", [128, 1], BF16))


def build_program(seqs, depth=4, n_layers_w=None, stop_after=None):
    import contextlib
    nt = sum(L for _, L in seqs)
    nseq = len(seqs)
    nw = depth if n_layers_w is None else n_layers_w
    na = (nw + 1) // 2
    nb_ = nw // 2
    nc = bass.Bass("TRN2", target_bir_lowering=False)
    di = lambda name, shape, dt=F32: nc.dram_tensor(name, list(shape), dt, kind="ExternalInput").ap()
    x = di("x", [nt, D])
    mem = di("mem", [nseq * N_MEM, D])
    norm_gains = di("norm_gains", [nw, 9, D])
    ffn_w_in = di("ffn_w_in", [nw, 2, D, 2 * DFF])
    ffn_w_out = di("ffn_w_out", [nw, 2, DFF, D])
    hg_w_in = di("hg_w_in", [na, D, 5 * D])
    hg_lb_logits = di("hg_lb_logits", [nw, D])
    hg_gnorm = di("hg_gnorm", [na, 128])
    hg_w_out = di("hg_w_out", [na, D, D])
    att_w_in = di("att_w_in", [max(nb_, 1), D, 9 * D])
    att_w_out = di("att_w_out", [max(nb_, 1), D, D])
    xa_w_q = di("xa_w_q", [nw, D, D])
    xa_w_kv = di("xa_w_kv", [nw, D, 2 * D])
    xa_w_o = di("xa_w_o", [nw, D, D])
    cds = {nm: di("c_" + nm, shp, dt_) for nm, shp, dt_ in CONST_SPECS}
    y = nc.dram_tensor("y", [nt, D], F32, kind="ExternalOutput").ap()
    ds = lambda name, shape, dt=F32: nc.dram_tensor(name, list(shape), dt, kind="Internal").ap()
    xs = ds("xs", [nt, D])
    scr = {}
    for nm in ("hqd", "hke", "hkd"):
        scr[nm] = ds(nm, [2, HD, 128, nt], BF16)
    scr["heb"] = ds("heb", [2, HD, 128, nt // CH])
    scr["hv"] = ds("hv", [nt, D], BF16)
    scr["hg"] = ds("hg", [nt, D], BF16)
    scr["hob"] = ds("hob", [nt, D])
    scr["hog"] = ds("hog", [nt, D], BF16)
    aqkv = ds("aqkv", [nt, 9 * D], BF16)
    ao = [ds(f"ao{i}", [nt, D]) for i in range(3)]
    al = [ds(f"al{i}", [nt, NH]) for i in range(3)]
    with contextlib.ExitStack() as st:
        consts = {}
        r = Rec()
        for nm, shp, dt_ in CONST_SPECS:
            if nm in ("dm", "rmask"):
                consts["d_" + nm] = cds[nm]
                continue
            consts[nm] = st.enter_context(nc.sbuf_tensor("k_" + nm, shp, dt_))
            r.add("sp", "dma_start", out=consts[nm][:], in_=cds[nm], w=[nm], dma=True, sem=nm)
        consts["lb"] = st.enter_context(nc.sbuf_tensor("k_lb", [128, nw, 8], F32))
        consts["oml"] = st.enter_context(nc.sbuf_tensor("k_oml", [128, nw, 8], F32))
        r.emit(nc, "pro")
        lb_prologue(nc, consts, hg_lb_logits, nw)
        stages = []
        for i in range(depth):
            g = norm_gains[i]
            j = i // 2
            stages.append(("ffn", i, 0))
            stages.append(("mix", i, 0))
            stages.append(("xa", i, 0))
            stages.append(("ffn", i, 1))
        if stop_after is not None:
            stages = stages[:stop_after]
        cur = x
        for si, (kind, i, k) in enumerate(stages):
            g = norm_gains[i]
            j = i // 2
            last = si == len(stages) - 1
            dst = y if last else xs
            nm = f"L{i}{kind}{k}"
            if kind == "ffn":
                ffn_stage(nc, nm, consts, cur, dst, nt, ffn_w_in[i, k], ffn_w_out[i, k], g[0 if k == 0 else 7], g[1 if k == 0 else 8])
            elif kind == "mix" and i % 2 == 0:
                h1_stage(nc, nm + "a", consts, cur, nt, hg_w_in[j], g[2], hg_gnorm[j], i, scr)
                h2_stage(nc, nm + "b", consts, seqs, scr)
                proj_out_stage(nc, nm + "c", consts, cur, dst, nt, scr["hog"], hg_w_out[j], g[3])
            elif kind == "mix":
                a1_stage(nc, nm + "a", consts, cur, nt, att_w_in[j], g[2], aqkv)
                a2_stage(nc, nm + "b", consts, seqs, aqkv, ao, al)
                a3_stage(nc, nm + "c", consts, cur, dst, nt, ao, al, att_w_out[j], g[3])
            else:
                xattn_stage(nc, nm, consts, cur, dst, seqs, mem, xa_w_q[i], xa_w_kv[i], xa_w_o[i], g[4], g[5], g[6])
            cur = dst
    return nc


N_CORES = 8
P_B, P_L = 16, 2048
S_B, S_L = 16, 4096
_PROG = {}


def kernel(x_prompt, x_sample, mem_prompt, mem_sample, norm_gains, ffn_w_in, ffn_w_out, hg_w_in, hg_lb_logits,
           hg_gnorm, hg_w_out, att_w_in, att_w_out, xa_w_q, xa_w_kv, xa_w_o):
    f = lambda a: np.ascontiguousarray(np.asarray(a, dtype=np.float32))
    x_prompt, x_sample, mem_prompt, mem_sample = f(x_prompt), f(x_sample), f(mem_prompt), f(mem_sample)
    ppc, spc = P_B // N_CORES, S_B // N_CORES
    seqs = []
    t = 0
    for _ in range(ppc):
        seqs.append((t, P_L))
        t += P_L
    for _ in range(spc):
        seqs.append((t, S_L))
        t += S_L
    if "nc" not in _PROG:
        _PROG["nc"] = build_program(seqs)
    nc = _PROG["nc"]
    shared = dict(norm_gains=f(norm_gains), ffn_w_in=f(ffn_w_in), ffn_w_out=f(ffn_w_out), hg_w_in=f(hg_w_in),
                  hg_lb_logits=f(hg_lb_logits), hg_gnorm=f(hg_gnorm), hg_w_out=f(hg_w_out), att_w_in=f(att_w_in),
                  att_w_out=f(att_w_out), xa_w_q=f(xa_w_q), xa_w_kv=f(xa_w_kv), xa_w_o=f(xa_w_o))
    shared.update(const_arrays())
    in_maps = []
    for c in range(N_CORES):
        xp = x_prompt[c * ppc:(c + 1) * ppc].reshape(-1, D)
        xsm = x_sample[c * spc:(c + 1) * spc].reshape(-1, D)
        mp = mem_prompt[c * ppc:(c + 1) * ppc].reshape(-1, D)
        ms = mem_sample[c * spc:(c + 1) * spc].reshape(-1, D)
        m = dict(shared)
        m["x"] = np.ascontiguousarray(np.concatenate([xp, xsm], axis=0))
        m["mem"] = np.ascontiguousarray(np.concatenate([mp, ms], axis=0))
        in_maps.append(m)
    res = run_bass_kernel_spmd(nc, in_maps, core_ids=list(range(N_CORES)))
    ys = [np.asarray(r["y"]) for r in res.results]
    npp = ppc * P_L
    y_prompt = np.stack([y[:npp].reshape(ppc, P_L, D) for y in ys]).reshape(P_B, P_L, D)
    y_sample = np.stack([y[npp:].reshape(spc, S_L, D) for y in ys]).reshape(S_B, S_L, D)
    return (y_prompt.astype(np.float32), y_sample.astype(np.float32))
```
